# Optimizing a Trainium2 kernel written in Bass

```python
import math
import jax, jax.numpy as jnp
from jax import lax
import numpy as np

D_MODEL = 1024
BATCH = 4
SEQ = 4096
DEPTH = 2

N_MIXERS = 2
FNET_GROUPS = 8
FNET_GROUP_DIM = D_MODEL // FNET_GROUPS
DIFF_HEADS = 8
DK = D_MODEL // (2 * DIFF_HEADS)
DV = 2 * DK
QK_WIDTH = DIFF_HEADS * 2 * DK
V_WIDTH = DIFF_HEADS * DV
Q_BLOCK = 128
ALIBI_MAX_BIAS = 8.0
D_FF = 2816
CONV_WIDTH = 3
LN_EPS = 1e-5
ALPHA = (2.0 * DEPTH) ** 0.25
BETA = (8.0 * DEPTH) ** -0.25

kernel_name = "hybrid_fnet_diffattn_encoder"


def layer_norm(x, g=None, b=None, eps=LN_EPS):
    xf = x.astype(jnp.float32)
    mu = jnp.mean(xf, axis=-1, keepdims=True)
    xc = xf - mu
    var = jnp.mean(jnp.square(xc), axis=-1, keepdims=True)
    y = xc * lax.rsqrt(var + eps)
    if g is not None:
        y = y * g.astype(jnp.float32) + b.astype(jnp.float32)
    return y.astype(x.dtype)


def modulate(x, shift, scale):
    return layer_norm(x) * (1 + scale[:, None, :]) + shift[:, None, :]


def fourier_mixer(h, w_out):
    B, S, D = h.shape
    hg = h.astype(jnp.float32).reshape(B, S, FNET_GROUPS, FNET_GROUP_DIM)
    f = jnp.fft.fft2(hg, axes=(1, 3), norm="ortho").real
    return f.reshape(B, S, D).astype(h.dtype) @ w_out


def diff_attention(h, w_in, lq1, lk1, lq2, lk2, subln_g, w_out, layer_idx):
    B, S, _ = h.shape
    qkv = h @ w_in
    q = qkv[..., :QK_WIDTH].astype(jnp.float32).reshape(B, S, DIFF_HEADS, 2, DK)
    k = qkv[..., QK_WIDTH:2 * QK_WIDTH].astype(jnp.float32).reshape(B, S, DIFF_HEADS, 2, DK)
    v = qkv[..., 2 * QK_WIDTH:].astype(jnp.float32).reshape(B, S, DIFF_HEADS, DV)

    lam_init = 0.8 - 0.6 * math.exp(-0.3 * layer_idx)
    lam = (jnp.exp(jnp.sum(lq1.astype(jnp.float32) * lk1.astype(jnp.float32)))
           - jnp.exp(jnp.sum(lq2.astype(jnp.float32) * lk2.astype(jnp.float32)))
           + lam_init)
    slopes = jnp.exp2(-ALIBI_MAX_BIAS * jnp.arange(1, DIFF_HEADS + 1, dtype=jnp.float32) / DIFF_HEADS)

    nb = S // Q_BLOCK
    qb = q.reshape(B, nb, Q_BLOCK, DIFF_HEADS, 2, DK).transpose(1, 0, 2, 3, 4, 5) * (DK ** -0.5)
    pos_k = jnp.arange(S, dtype=jnp.float32)

    def one_block(args):
        q_blk, blk = args
        pos_q = (blk * Q_BLOCK + jnp.arange(Q_BLOCK)).astype(jnp.float32)
        dist = jnp.abs(pos_q[:, None] - pos_k[None, :])
        bias = -slopes[:, None, None] * dist
        s = jnp.einsum('bqhcd,bkhcd->bhcqk', q_blk, k) + bias[None, :, None]
        p = jax.nn.softmax(s, axis=-1)
        a = p[:, :, 0] - lam * p[:, :, 1]
        return jnp.einsum('bhqk,bkhd->bqhd', a, v)

    o = lax.map(one_block, (qb, jnp.arange(nb)))
    o = o.transpose(1, 0, 2, 3, 4).reshape(B, S, DIFF_HEADS, DV)
    o = o * lax.rsqrt(jnp.mean(jnp.square(o), axis=-1, keepdims=True) + LN_EPS)
    o = o * subln_g.astype(jnp.float32) * (1.0 - lam_init)
    return o.reshape(B, S, V_WIDTH).astype(h.dtype) @ w_out


def conv_gated_ffn(h, w_up, conv_w, conv_b, w_down):
    u = h @ w_up
    up = jnp.pad(u, ((0, 0), (1, 1), (0, 0)))
    u = up[:, :-2] * conv_w[0] + up[:, 1:-1] * conv_w[1] + up[:, 2:] * conv_w[2] + conv_b
    val, gate = u[..., :D_FF], u[..., D_FF:]
    return (jax.nn.gelu(gate, approximate=False) * val) @ w_down


def setup_inputs(seed: int = 0) -> dict:
    key = jax.random.key(seed)
    keys = iter(jax.random.split(key, 64))

    def nrm(shape, scale):
        return jax.random.normal(next(keys), shape, jnp.float32) * scale

    def gain(n):
        return 1.0 + nrm((n,), 0.02)

    d = {}
    d["x"] = nrm((BATCH, SEQ, D_MODEL), 1.0)
    d["c"] = nrm((BATCH, D_MODEL), 1.0)

    def ada(prefix):
        d[prefix + "ada_w"] = nrm((D_MODEL, 6 * D_MODEL), 0.5 * D_MODEL ** -0.5)
        d[prefix + "ada_b"] = nrm((6 * D_MODEL,), 0.01)

    def ln(name):
        d[name + "_g"] = gain(D_MODEL)
        d[name + "_b"] = nrm((D_MODEL,), 0.01)

    def ffn(prefix):
        d[prefix + "ffn_w_up"] = nrm((D_MODEL, 2 * D_FF), BETA * D_MODEL ** -0.5)
        d[prefix + "ffn_conv_w"] = nrm((CONV_WIDTH, 2 * D_FF), CONV_WIDTH ** -0.5)
        d[prefix + "ffn_conv_b"] = nrm((2 * D_FF,), 0.01)
        d[prefix + "ffn_w_down"] = nrm((D_FF, D_MODEL), BETA * D_FF ** -0.5)

    ada("l0_")
    d["l0_fnet_w_out"] = nrm((D_MODEL, D_MODEL), BETA * D_MODEL ** -0.5)
    ln("l0_ln_mix")
    ffn("l0_")
    ln("l0_ln_ffn")

    ada("l1_")
    w_qk = nrm((D_MODEL, 2 * QK_WIDTH), D_MODEL ** -0.5)
    w_v = nrm((D_MODEL, V_WIDTH), BETA * D_MODEL ** -0.5)
    d["l1_attn_w_in"] = jnp.concatenate([w_qk, w_v], axis=1)
    d["l1_attn_lambda_q1"] = nrm((DK,), 0.1)
    d["l1_attn_lambda_k1"] = nrm((DK,), 0.1)
    d["l1_attn_lambda_q2"] = nrm((DK,), 0.1)
    d["l1_attn_lambda_k2"] = nrm((DK,), 0.1)
    d["l1_attn_subln_g"] = gain(DV)
    d["l1_attn_w_out"] = nrm((V_WIDTH, D_MODEL), BETA * V_WIDTH ** -0.5)
    ln("l1_ln_mix")
    ffn("l1_")
    ln("l1_ln_ffn")
    return d


def reference(x, c,
              l0_ada_w, l0_ada_b, l0_fnet_w_out, l0_ln_mix_g, l0_ln_mix_b,
              l0_ffn_w_up, l0_ffn_conv_w, l0_ffn_conv_b, l0_ffn_w_down, l0_ln_ffn_g, l0_ln_ffn_b,
              l1_ada_w, l1_ada_b, l1_attn_w_in, l1_attn_lambda_q1, l1_attn_lambda_k1,
              l1_attn_lambda_q2, l1_attn_lambda_k2, l1_attn_subln_g, l1_attn_w_out,
              l1_ln_mix_g, l1_ln_mix_b,
              l1_ffn_w_up, l1_ffn_conv_w, l1_ffn_conv_b, l1_ffn_w_down, l1_ln_ffn_g, l1_ln_ffn_b):
    ada_w = (l0_ada_w, l1_ada_w)
    ada_b = (l0_ada_b, l1_ada_b)
    ln_mix = ((l0_ln_mix_g, l0_ln_mix_b), (l1_ln_mix_g, l1_ln_mix_b))
    ln_ffn = ((l0_ln_ffn_g, l0_ln_ffn_b), (l1_ln_ffn_g, l1_ln_ffn_b))
    ffn_p = ((l0_ffn_w_up, l0_ffn_conv_w, l0_ffn_conv_b, l0_ffn_w_down),
             (l1_ffn_w_up, l1_ffn_conv_w, l1_ffn_conv_b, l1_ffn_w_down))
    mixer_a_p = ((l0_fnet_w_out,),)
    mixer_b_p = ((l1_attn_w_in, l1_attn_lambda_q1, l1_attn_lambda_k1, l1_attn_lambda_q2,
                  l1_attn_lambda_k2, l1_attn_subln_g, l1_attn_w_out),)

    c_act = jax.nn.silu(c)
    for i in range(DEPTH):
        mod = c_act @ ada_w[i] + ada_b[i]
        sh1, sc1, g1, sh2, sc2, g2 = jnp.split(mod, 6, axis=-1)

        h = modulate(x, sh1, sc1)
        if i % N_MIXERS == 0:
            y = fourier_mixer(h, *mixer_a_p[i // N_MIXERS])
        else:
            y = diff_attention(h, *mixer_b_p[i // N_MIXERS], layer_idx=i)
        x = layer_norm(ALPHA * x + g1[:, None, :] * y, *ln_mix[i])

        h = modulate(x, sh2, sc2)
        y = conv_gated_ffn(h, *ffn_p[i])
        x = layer_norm(ALPHA * x + g2[:, None, :] * y, *ln_ffn[i])
    return x
```

```python
import contextlib
import math

import ml_dtypes
import numpy as np

import concourse.bass as bass
import concourse.mybir as mybir
from concourse.bass_utils import run_bass_kernel_spmd

F32 = mybir.dt.float32
BF16 = mybir.dt.bfloat16
U8 = mybir.dt.uint8
ALU = mybir.AluOpType
AF = mybir.ActivationFunctionType
AX = mybir.AxisListType
NPBF = ml_dtypes.bfloat16

D = 1024
S = 4096
T = 2048
NT = 16
DFF = 2816
NPAIR = 22
ALPHA = 4.0 ** 0.25
EPS = 1e-5
LAM_INIT = 0.8 - 0.6 * math.exp(-0.3 * 1)
ENGS = ("pe", "act", "dve", "pool", "sp")


class _Op:
    __slots__ = ("eng", "emit", "deps", "idx", "signal", "dma", "sem", "val", "prev", "cc")

    def __init__(self, eng, emit, dma):
        self.eng = eng
        self.emit = emit
        self.deps = set()
        self.signal = False
        self.dma = dma
        self.cc = False
        self.sem = None
        self.val = 0
        self.prev = 0


class Prog:
    def __init__(self, nc, n_dma_sems=14):
        self.nc = nc
        self.ops = []
        self.last_w = {}
        self.readers = {}
        self.n_dma_sems = n_dma_sems
        self.since_barrier = []

    def add(self, eng, emit, reads=(), writes=(), dma=False, cc=False):
        op = _Op(eng, emit, dma or cc)
        op.cc = cc
        op.idx = len(self.ops)
        ops = self.ops
        for r in reads:
            w = self.last_w.get(r)
            if w is not None:
                op.deps.add(w)
            if isinstance(r, tuple) and r[0] == "ps":
                for pr in self.readers.get(r, ()):
                    if ops[pr].eng != eng:
                        op.deps.add(pr)
        for wkey in writes:
            w = self.last_w.get(wkey)
            if w is not None:
                wo = ops[w]
                if wo.dma or dma or wo.eng != eng:
                    op.deps.add(w)
            for r in self.readers.get(wkey, ()):
                ro = ops[r]
                if ro.dma or dma or ro.eng != eng:
                    op.deps.add(r)
        for r in reads:
            self.readers.setdefault(r, []).append(op.idx)
        for wkey in writes:
            self.last_w[wkey] = op.idx
            self.readers[wkey] = []
        op.deps.discard(op.idx)
        ops.append(op)
        self.since_barrier.append(op.idx)
        return op.idx

    def barrier(self):
        prev = [i for i in self.since_barrier if not self.ops[i].cc]
        self.since_barrier = []
        for e in ENGS:
            op = _Op(e, None, False)
            op.idx = len(self.ops)
            op.deps = set(prev)
            self.ops.append(op)

    def build(self, final_wait_eng="sp"):
        nc = self.nc
        ops = self.ops
        for op in ops:
            for d in op.deps:
                ops[d].signal = True
        EPOCH = 384
        n_sig = {e: 0 for e in ENGS}
        n_dma = {e: 0 for e in ("sp", "act", "pool")}
        for op in ops:
            if op.emit is None:
                continue
            if op.cc:
                pass
            elif op.dma:
                n_dma[op.eng] += 1
            elif op.signal:
                n_sig[op.eng] += 1
        with contextlib.ExitStack() as es:
            esem = {e: [es.enter_context(nc.semaphore("s_%s%d" % (e, i)))
                        for i in range((n_sig[e] + EPOCH - 1) // EPOCH)] for e in ENGS}
            npool = {e: (max(self.n_dma_sems, (n_dma[e] * 16 + EPOCH - 1) // EPOCH) if n_dma[e] else 0) for e in n_dma}
            dsem = {e: [es.enter_context(nc.semaphore("d_%s%d" % (e, i))) for i in range(npool[e])] for e in n_dma}
            ticks = {e: 0 for e in ENGS}
            dcnt = {e: 0 for e in dsem}
            dval = {e: [0] * npool[e] for e in dsem}
            for op in ops:
                if op.emit is None:
                    continue
                if op.cc:
                    op.sem = es.enter_context(nc.semaphore("cc%d" % op.idx))
                    op.prev = 0
                    op.val = 1
                elif op.dma:
                    i = dcnt[op.eng] % npool[op.eng]
                    dcnt[op.eng] += 1
                    op.sem = dsem[op.eng][i]
                    op.prev = dval[op.eng][i]
                    dval[op.eng][i] += 16
                    op.val = dval[op.eng][i]
                elif op.signal:
                    t = ticks[op.eng]
                    ticks[op.eng] += 1
                    op.sem = esem[op.eng][t // EPOCH]
                    op.val = t % EPOCH + 1
            per_eng = {e: [op for op in ops if op.eng == e] for e in ENGS}
            block = es.enter_context(nc.Block())

            def run(ename, eng):
                waited = {}

                def wait(sem, val):
                    k = id(sem)
                    if waited.get(k, 0) >= val:
                        return
                    waited[k] = val
                    eng.wait_ge(sem, val)

                for op in per_eng[ename]:
                    need = {}
                    for d in op.deps:
                        do = ops[d]
                        if do.sem is None:
                            continue
                        k = id(do.sem)
                        if k not in need or need[k][1] < do.val:
                            need[k] = (do.sem, do.val)
                    for sem, val in need.values():
                        wait(sem, val)
                    if op.emit is None:
                        continue
                    if op.dma and op.prev > 0:
                        wait(op.sem, op.prev)
                    ins = op.emit(eng)
                    if op.cc:
                        ins.then_inc(op.sem)
                    elif op.dma:
                        ins.then_inc(op.sem, 16)
                    elif op.signal:
                        ins.then_inc(op.sem, 1)
                if ename == final_wait_eng:
                    for e in dsem:
                        for i in range(npool[e]):
                            if dval[e][i] > 0:
                                wait(dsem[e][i], dval[e][i])

            @block.sync
            def _(eng):
                run("sp", eng)

            @block.scalar
            def _(eng):
                run("act", eng)

            @block.vector
            def _(eng):
                run("dve", eng)

            @block.gpsimd
            def _(eng):
                run("pool", eng)

            @block.tensor
            def _(eng):
                run("pe", eng)


class Ring:
    def __init__(self, name, aps, extra=()):
        self.items = [(ap, (name, j)) for j, ap in enumerate(aps)] + list(extra)
        self.i = 0

    def next(self):
        j = self.i % len(self.items)
        self.i += 1
        return self.items[j]


def emit_skewed(tiles):
    nseg = max(len(t) for t in tiles)
    for step in range(len(tiles) + nseg - 1):
        for sgi in range(nseg - 1, -1, -1):
            t = step - sgi
            if 0 <= t < len(tiles) and sgi < len(tiles[t]):
                tiles[t][sgi]()


class Cx:
    def __init__(self, nc, arena, arena_bytes, ps):
        self.nc = nc
        self.P = Prog(nc)
        self.arena = arena
        self.arena_bytes = arena_bytes
        self.top = 0
        self.peak = 0
        self.ps = ps

    def take(self, nbytes, dtype):
        off = (self.top + 63) // 64 * 64
        self.top = off + nbytes
        self.peak = max(self.peak, self.top)
        assert self.top <= self.arena_bytes, ("SBUF arena overflow", self.top)
        return self.arena[:, off:off + nbytes].bitcast(dtype)

    def f32(self, n):
        return self.take(4 * n, F32)

    def bf(self, n):
        return self.take(2 * n, BF16)

    def mark(self):
        return self.top

    def release(self, m):
        self.top = m

    def bank(self, b, nb=1):
        return self.ps[:, 512 * b:512 * (b + nb)]

    def dma(self, q, out, in_, reads=(), writes=()):
        self.P.add(q, lambda e: e.dma_start(out=out, in_=in_), reads, writes, dma=True)

    def act(self, out, in_, func, reads, writes, bias=None, scale=None, accum_out=None):
        kw = {}
        if bias is not None:
            kw["bias"] = bias
        if scale is not None:
            kw["scale"] = scale
        if accum_out is not None:
            kw["accum_out"] = accum_out
        self.P.add("act", lambda e: e.activation(out=out, in_=in_, func=func, **kw), reads, writes)

    def tt(self, eng, out, in0, in1, op, reads, writes):
        self.P.add(eng, lambda e: e.tensor_tensor(out=out, in0=in0, in1=in1, op=op), reads, writes)

    def ts(self, eng, out, in0, s1, s2, op0, op1, reads, writes):
        if s2 is None:
            self.P.add(eng, lambda e: e.tensor_scalar(out=out, in0=in0, scalar1=s1, scalar2=None, op0=op0), reads, writes)
        else:
            self.P.add(eng, lambda e: e.tensor_scalar(out=out, in0=in0, scalar1=s1, scalar2=s2, op0=op0, op1=op1), reads, writes)

    def stt(self, eng, out, in0, scalar, in1, op0, op1, reads, writes):
        self.P.add(eng, lambda e: e.scalar_tensor_tensor(out=out, in0=in0, scalar=scalar, in1=in1, op0=op0, op1=op1), reads, writes)

    def copy(self, eng, out, in_, reads, writes):
        if eng == "act":
            self.P.add("act", lambda e: e.copy(out=out, in_=in_), reads, writes)
        else:
            self.P.add(eng, lambda e: e.tensor_copy(out=out, in_=in_), reads, writes)

    def memset(self, eng, ap, val, writes):
        self.P.add(eng, lambda e: e.memset(ap, val), (), writes)

    def mms(self, lst, reads, writes):
        lst = list(lst)

        def f(e):
            ins = None
            for t in lst:
                o, l, r, st, sp = t[:5]
                if len(t) > 5 and t[5]:
                    ins = e.matmul(o, lhsT=l, rhs=r, start=st, stop=sp, skip_group_check=True)
                else:
                    ins = e.matmul(o, lhsT=l, rhs=r, start=st, stop=sp)
            return ins

        self.P.add("pe", f, reads, writes)

    def transposes(self, lst, ident, reads, writes):
        lst = list(lst)

        def f(e):
            ins = None
            for (o, i) in lst:
                ins = e.transpose(o, i, ident)
            return ins

        self.P.add("pe", f, reads, writes)


def load_consts(cx, io):
    c = {}
    c["ident"] = cx.bf(128)
    cx.dma("sp", c["ident"], io["ident"], (), ["ident"])
    c["ones"] = cx.bf(128)
    cx.memset("pool", c["ones"], 1.0, ["ones"])
    c["eps"] = cx.f32(1)
    cx.memset("pool", c["eps"], EPS, ["epsc"])
    cx.eps = c["eps"]
    return c


def ada_setup(cx, io, consts):
    cs_raw = cx.f32(8)
    cs = cx.f32(8)
    cbc = cx.bf(8 * 128).rearrange("p (a b) -> p a b", b=128)
    cx.dma("sp", cs_raw, io["c_t"], (), ["cs_raw"])
    cx.act(cs, cs_raw, AF.Silu, ["cs_raw"], ["cs"])
    for kc in range(8):
        cx.ts("dve", cbc[:, kc, :], consts["ones"], cs[:, kc:kc + 1], None, ALU.mult, None,
              ["ones", "cs"], [("cbc", kc)])
    consts["cbc"] = cbc


def ada_vectors(cx, io, consts, layer, items, wring, bring, banks):
    cbc = consts["cbc"]
    ada_w = io["ada_w%d" % layer]
    ada_b = io["ada_b%d" % layer]
    n = 0
    for (ci, dest, dkey, plus_one) in items:
        for half in range(2):
            c0 = ci * 1024 + half * 512
            wt, wk = wring.next()
            bt, bk = bring.next()
            wt3 = wt.rearrange("p (a b) -> p a b", b=512)
            cx.dma("pool", wt, ada_w[ci * 2 + half], (), [wk])
            cx.dma("sp", bt, ada_b[c0:c0 + 512].partition_broadcast(128), (), [bk])
            b = banks[n % len(banks)]
            n += 1
            pb = cx.bank(b)
            cx.mms([(pb, cbc[:, kc, :], wt3[:, kc, :], kc == 0, kc == 7) for kc in range(8)],
                   [wk] + [("cbc", kc) for kc in range(8)], [("ps", b)])
            cx.stt("dve", dest[:, half * 512:(half + 1) * 512], pb, 1.0 if plus_one else 0.0, bt,
                   ALU.add, ALU.add, [("ps", b), bk], [(dkey, half)])


def bc_load(cx, dest, dkey, vec_ap):
    cx.dma("sp", dest, vec_ap.partition_broadcast(128), (), [(dkey, 0), (dkey, 1)])


def ln_stats(cx, src, rows, skey, st, reads):
    stats, mv, rstd, nmr = st["stats"], st["mv"], st["rstd"], st["nmr"]
    k = st["key"]
    cx.P.add("dve", lambda e: e.bn_stats(out=stats[:rows, 0:6], in_=src[:rows, 0:512]), reads, [(k, "s0")])
    cx.P.add("dve", lambda e: e.bn_stats(out=stats[:rows, 6:12], in_=src[:rows, 512:1024]), reads, [(k, "s1")])
    cx.P.add("dve", lambda e: e.bn_aggr(out=mv[:rows, :], in_=stats[:rows, :]), [(k, "s0"), (k, "s1")], [(k, "mv")])
    cx.act(rstd[:rows, :], mv[:rows, 1:2], AF.Sqrt, [(k, "mv"), "epsc"], [(k, "rstd")], bias=cx.eps[:rows, :], scale=1.0)
    cx.P.add("dve", lambda e: e.reciprocal(out=rstd[:rows, :], in_=rstd[:rows, :]), [(k, "rstd")], [(k, "rstd")])
    cx.ts("dve", nmr[:rows, :], mv[:rows, 0:1], rstd[:rows, :], -1.0, ALU.mult, ALU.mult,
          [(k, "mv"), (k, "rstd")], [(k, "nmr")])


def make_stat_ring(cx, name, n):
    sts = []
    for i in range(n):
        sts.append({"stats": cx.f32(12), "mv": cx.f32(2), "rstd": cx.f32(1), "nmr": cx.f32(1), "key": (name, i)})
    return sts


class Blocks:
    def __init__(self, cx, bc, n_xt=3, n_nt=2, n_tt=2, xt_extra=(), tt_extra=()):
        self.cx = cx
        self.bc = bc
        self.xt = Ring("xt", [cx.f32(1024) for _ in range(n_xt)], xt_extra)
        self.nt = Ring("nt", [cx.f32(1024) for _ in range(n_nt)])
        self.tt_ = Ring("tt", [cx.f32(1024) for _ in range(n_tt)], tt_extra)
        self.st = make_stat_ring(cx, "st", 6)
        self.sti = 0

    def _st(self):
        s = self.st[self.sti % len(self.st)]
        self.sti += 1
        return s

    def load_x(self, src, rows=128):
        cx = self.cx
        xt, xk = self.xt.next()
        if src[0] == "d":
            rk = list(src[2]) if len(src) > 2 else []
            cx.dma("sp", xt[:rows, :], src[1], rk, [xk])
        else:
            rk = list(src[4]) if len(src) > 4 else []
            tmp, tk = self.tt_.next()
            cx.dma("sp", xt[:rows, :], src[1], rk, [xk])
            cx.dma("sp", tmp[:rows, :], src[2], rk, [tk])
            m = src[3]
            cx.act(tmp[:rows, :], tmp[:rows, :], AF.Copy, [tk, "mask"], [tk], scale=m[:rows, 1:2])
            cx.stt("dve", xt[:rows, :], xt[:rows, :], m[:rows, 0:1], tmp[:rows, :], ALU.mult, ALU.add,
                   [xk, tk, "mask"], [xk])
        return xt, xk

    def pro_a(self, xt, xk, rows):
        st = self._st()
        ln_stats(self.cx, xt, rows, None, st, [xk])
        return st

    def pro_b(self, xt, xk, rows, st, out, okeys):
        nn = self.pro_b1(xt, xk, rows, st)
        self.pro_b2(nn, rows, out, okeys)

    def pro_b1(self, xt, xk, rows, st):
        cx = self.cx
        n, nk = self.nt.next()
        k = st["key"]
        cx.act(n[:rows, :], xt[:rows, :], AF.Identity, [xk, (k, "rstd"), (k, "nmr")], [nk],
               bias=st["nmr"][:rows, :], scale=st["rstd"][:rows, :])
        return n, nk

    def pro_b2(self, nn, rows, out, okeys):
        cx = self.cx
        n, nk = nn
        cx.tt("dve", n[:rows, :], n[:rows, :], self.bc["sc1"][:rows, :], ALU.mult,
              [nk, ("sc1", 0), ("sc1", 1)], [nk])
        cx.tt("pool", out, n[:rows, :], self.bc["sh"][:rows, :], ALU.add,
              [nk, ("sh", 0), ("sh", 1)], okeys)

    def epi_a(self, y, ykeys):
        t, tk = self.tt_.next()
        self.cx.tt("dve", t, y, self.bc["g"], ALU.mult, list(ykeys) + [("g", 0), ("g", 1)], [tk])
        return t, tk

    def epi_b1(self, t, tk, xt, xk):
        cx = self.cx
        cx.stt("dve", t, xt, ALPHA, t, ALU.mult, ALU.add, [xk, tk], [tk])
        st = self._st()
        ln_stats(cx, t, 128, None, st, [tk])
        return st

    def epi_b2(self, t, tk, st, out_dram, halo_out=None):
        cx = self.cx
        k = st["key"]
        n, nk = self.nt.next()
        cx.act(n, t, AF.Identity, [tk, (k, "rstd"), (k, "nmr")], [nk], bias=st["nmr"], scale=st["rstd"])
        cx.tt("pool", n, n, self.bc["gam"], ALU.mult, [nk, ("gam", 0), ("gam", 1)], [nk])
        self.nb2 = getattr(self, "nb2", 0) + 1
        cx.tt("pool" if self.nb2 % 2 else "dve", n, n, self.bc["bet"], ALU.add, [nk, ("bet", 0), ("bet", 1)], [nk])
        oap, okey = out_dram
        cx.dma("sp", oap, n, [nk], [okey])
        if halo_out is not None:
            cx.dma("sp", halo_out, n[127:128, :], [nk], ["halo_out"])

    def epilogue_group(self, items):
        sts = []
        for it in items:
            it["x"] = self.load_x(it["src"])
        for it in items:
            it["t"] = self.epi_a(it["y"], it["ykeys"])
        tiles = []
        for it in items:
            def b1(it=it):
                it["st"] = self.epi_b1(it["t"][0], it["t"][1], it["x"][0], it["x"][1])

            def b2(it=it):
                self.epi_b2(it["t"][0], it["t"][1], it["st"], it["out"], it.get("halo_out"))
                if it.get("hook") is not None:
                    it["hook"]()
            tiles.append([b1, b2])
        emit_skewed(tiles)


def rows_of(src, r0, n):
    if src[0] == "f":
        return src[1](r0, n)
    if src[0] == "d":
        return ("d", src[1][r0:r0 + n, :], src[2] if len(src) > 2 else [])
    return ("m", src[1][r0:r0 + n, :], src[2][r0:r0 + n, :], src[3], src[4] if len(src) > 4 else [])


def stage_ffn(cx, io, consts, layer, x_own, halo, x_out, halo_out=None, tile_hook=None):
    m0 = cx.mark()
    bc = {k: cx.f32(1024) for k in ("sh", "sc1", "g", "gam", "bet")}
    av = Ring("av", [cx.f32(1024) for _ in range(2)])
    ag = Ring("ag", [cx.f32(1024) for _ in range(2)])
    gl = Ring("gl", [cx.f32(1024) for _ in range(1)])
    B = Blocks(cx, bc, n_xt=2, n_nt=2, n_tt=2, xt_extra=av.items, tt_extra=ag.items)
    hb = Ring("hb", [cx.bf(1024) for _ in range(2)])
    hT = cx.bf(2 * 8 * 1026).rearrange("p (h a b) -> p h a b", h=2, a=8)
    wup = Ring("wup", [cx.bf(2 * 8 * 128) for _ in range(3)])
    cwb = cx.f32(44 * 4).rearrange("p (a b) -> p a b", b=4)
    wd = cx.bf(NPAIR * 1024).rearrange("p (a b) -> p a b", b=1024)
    gT = cx.bf(NPAIR * 1024).rearrange("p (a b) -> p a b", b=1024)
    m1 = cx.mark()
    cx.release(m1 - NPAIR * 1024 * 2)
    wring = Ring("adaw", [cx.bf(8 * 512) for _ in range(2)])
    bring = Ring("adab", [cx.f32(512) for _ in range(2)])
    cx.release(m1)

    ada_vectors(cx, io, consts, layer, [(3, bc["sh"], "sh", False), (4, bc["sc1"], "sc1", True),
                                         (5, bc["g"], "g", False)], wring, bring, [6, 7])
    bc_load(cx, bc["gam"], "gam", io["ln_ffn_g%d" % layer])
    bc_load(cx, bc["bet"], "bet", io["ln_ffn_b%d" % layer])
    cx.dma("sp", cwb, io["cwb%d" % layer], (), ["cwb"])
    w_down = io["w_down%d" % layer]

    ident = consts["ident"]
    cx.memset("pool", hT[:, 0, :, 0:1], 0.0, [("hT", 0, "pad")])
    tb = [6, 7]
    ptiles = []
    for i in range(NT + 1):
        stt_ = {}

        def s0(i=i, stt_=stt_):
            rows = 128 if i < NT else 1
            src = rows_of(x_own, i * 128, 128) if i < NT else halo
            stt_["x"] = B.load_x(src, rows)
            stt_["st"] = B.pro_a(stt_["x"][0], stt_["x"][1], rows)

        def s1(i=i, stt_=stt_):
            rows = 128 if i < NT else 1
            stt_["n"] = B.pro_b1(stt_["x"][0], stt_["x"][1], rows, stt_["st"])

        def s1b(i=i, stt_=stt_):
            rows = 128 if i < NT else 1
            stt_["h"] = hb.next()
            B.pro_b2(stt_["n"], rows, stt_["h"][0][:rows, :], [stt_["h"][1]])

        def s2(i=i, stt_=stt_):
            rows = 128 if i < NT else 1
            h_, hk = stt_["h"]
            b = tb[i % 2]
            pT = cx.bank(b).bitcast(BF16)[:, 0:1024].rearrange("p (a b) -> p a b", b=128)
            cx.transposes([(pT[:, dc, 0:rows], h_[:rows, dc * 128:(dc + 1) * 128]) for dc in range(8)],
                          ident[:rows, :rows], [hk, "ident"], [("ps", b)])
            eng = "act"
            if i < NT:
                half = i // 8
                j0 = 128 * (i % 8) + 1
                cx.copy(eng, hT[:, half, :, j0:j0 + 128], pT, [("ps", b)], [("hT", half, i)])
                if i == 7:
                    cx.copy(eng, hT[:, 1, :, 0:1], pT[:, :, 127:128], [("ps", b)], [("hT", 1, "l")])
                if i == 8:
                    cx.copy(eng, hT[:, 0, :, 1025:1026], pT[:, :, 0:1], [("ps", b)], [("hT", 0, "r")])
            else:
                cx.copy(eng, hT[:, 1, :, 1025:1026], pT[:, :, 0:1], [("ps", b)], [("hT", 1, "r")])

        ptiles.append([s0, s1, s1b, (lambda: None), s2])
    emit_skewed(ptiles)
    hT_keys = {0: [("hT", 0, "pad"), ("hT", 0, "r")] + [("hT", 0, i) for i in range(8)],
               1: [("hT", 1, "l"), ("hT", 1, "r")] + [("hT", 1, i) for i in range(8, 16)]}

    w_up = io["w_up%d" % layer]
    splits = [(255, 512, 0, 257), (512, 1024, 257, 769), (1024, 1281, 769, 1026)]
    order = [(hf, i) for hf in range(2) for i in range(NPAIR)]
    issued = {}
    PF = 2

    def issue(k):
        if k < len(order) and k not in issued:
            wt_, wk_ = wup.next()
            cx.dma("pool", wt_, w_up[order[k][1]], (), [wk_])
            issued[k] = (wt_, wk_)

    for half in range(2):
        for i in range(NPAIR):
            kk = half * NPAIR + i
            for k2 in range(kk, kk + PF + 1):
                issue(k2)
            if half == 0 and i % 2 == 0:
                cx.dma("pool", wd[:, i:i + 2, :], w_down[:, i:i + 2, :], (), [("wd", i), ("wd", i + 1)])
            wt, wk = issued[kk]
            wt4 = wt.rearrange("p (v a b) -> p v a b", v=2, a=8)
            regs = []
            for vg in range(2):
                base = vg * 1536
                pkeys = [("ps", 3 * vg + j) for j in range(3)]
                lst = []
                for dc in range(8):
                    for (p0, p1, h0, h1) in splits:
                        lst.append((cx.ps[:, base + p0:base + p1], wt4[:, vg, dc, :], hT[:, half, dc, h0:h1],
                                    dc == 0, dc == 7))
                cx.mms(lst, [wk] + hT_keys[half], pkeys)
                regs.append((base, pkeys))
            outs = []
            for vg, ring in ((0, av), (1, ag)):
                base, pkeys = regs[vg]
                a, ak = ring.next()
                ci = vg * NPAIR + i
                u = cx.ps
                cx.act(a, u[:, base + 256:base + 1280], AF.Identity, pkeys + ["cwb"], [ak],
                       bias=cwb[:, ci, 3:4], scale=cwb[:, ci, 1:2])
                cx.stt("dve", a, u[:, base + 255:base + 1279], cwb[:, ci, 0:1], a, ALU.mult, ALU.add,
                       pkeys + ["cwb", ak], [ak])
                cx.stt("dve", a, u[:, base + 257:base + 1281], cwb[:, ci, 2:3], a, ALU.mult, ALU.add,
                       pkeys + ["cwb", ak], [ak])
                outs.append((a, ak))
            g_, gk = gl.next()
            cx.act(g_, outs[1][0], AF.Gelu, [outs[1][1]], [gk])
            alias = [("adaw", 0), ("adaw", 1), ("adab", 0), ("adab", 1)] if (half == 0 and i < 10) else []
            cx.tt("pool", gT[:, i, :], outs[0][0], g_, ALU.mult, [outs[0][1], gk], [("gT", i)] + alias)
        for grp in range(2):
            lst = []
            for nc_ in range(NPAIR):
                for tl in range(4):
                    c0 = (grp * 4 + tl) * 128
                    for nh in range(2):
                        lst.append((cx.bank(tl * 2 + nh), gT[:, nc_, c0:c0 + 128], wd[:, nc_, nh * 512:(nh + 1) * 512],
                                    nc_ == 0, nc_ == NPAIR - 1))
            cx.mms(lst, [("gT", i) for i in range(NPAIR)] + [("wd", i) for i in range(NPAIR)],
                   [("ps", b) for b in range(8)])
            items = []
            for tl in range(4):
                r0 = (half * 8 + grp * 4 + tl) * 128
                items.append(dict(y=cx.ps[:, tl * 1024:(tl + 1) * 1024], ykeys=[("ps", 2 * tl), ("ps", 2 * tl + 1)],
                                  src=rows_of(x_own, r0, 128), out=x_out(r0),
                                  halo_out=halo_out if r0 == T - 128 else None,
                                  hook=(lambda r0=r0: tile_hook(r0)) if tile_hook is not None else None))
            B.epilogue_group(items)
    cx.P.barrier()
    cx.release(m0)


def stage_fnet(cx, io, consts, x_own, x_par, x_out, halo_out=None):
    m0 = cx.mark()
    bc = {k: cx.f32(1024) for k in ("sh", "sc1", "g", "gam", "bet")}
    B = Blocks(cx, bc, n_xt=4, n_nt=2, n_tt=4)
    h_all = cx.bf(32 * 1024).rearrange("p (a b) -> p a b", b=1024)
    dring = Ring("dft", [cx.bf(4 * 512) for _ in range(4)])
    FT = [cx.bf(8 * 512).rearrange("p (a b) -> p a b", b=512) for _ in range(2)]
    GTb = cx.bf(8 * 512).rearrange("p (a b) -> p a b", b=512)
    wo = cx.bf(8 * 1024).rearrange("p (a b) -> p a b", b=1024)
    c128 = cx.bf(2 * 128).rearrange("p (a b) -> p a b", b=128)
    wring = Ring("adaw", [cx.bf(8 * 512) for _ in range(2)])
    bring = Ring("adab", [cx.f32(512) for _ in range(2)])

    ada_vectors(cx, io, consts, 0, [(0, bc["sh"], "sh", False), (1, bc["sc1"], "sc1", True),
                                     (2, bc["g"], "g", False)], wring, bring, [6, 7])
    bc_load(cx, bc["gam"], "gam", io["ln_mix_g0"])
    bc_load(cx, bc["bet"], "bet", io["ln_mix_b0"])
    cx.dma("sp", c128, io["c128"], (), ["c128"])
    for q in range(0, 8, 2):
        cx.dma("pool", wo[:, q:q + 2, :], io["fnet_w_out"][:, q:q + 2, :], (), [("wo", q), ("wo", q + 1)])

    ptiles = []
    for et in range(32):
        stt_ = {}

        def s0(et=et, stt_=stt_):
            src = rows_of(x_own, et * 128, 128) if et < NT else rows_of(x_par, (et - NT) * 128, 128)
            stt_["x"] = B.load_x(src)
            stt_["st"] = B.pro_a(stt_["x"][0], stt_["x"][1], 128)

        def s1(et=et, stt_=stt_):
            stt_["n"] = B.pro_b1(stt_["x"][0], stt_["x"][1], 128, stt_["st"])

        def s1b(et=et, stt_=stt_):
            B.pro_b2(stt_["n"], 128, h_all[:, et, :], [("h_all", et)])

        ptiles.append([s0, s1, s1b])
    emit_skewed(ptiles)
    hkeys = [("h_all", et) for et in range(32)]

    dftm = [io["dftc"], io["dfts"]]
    allb = [("ps", b) for b in range(8)]
    ne = 0
    for kc in range(4):
        for cs in range(2):
            for sg in range(8):
                dt_, dk = dring.next()
                cx.dma("sp", dt_, dftm[cs][kc, sg], (), [dk])
                dt3 = dt_.rearrange("p (a b) -> p a b", b=512)
                lst = []
                for sc in range(4):
                    sch = sg * 4 + sc
                    for dc in range(8):
                        lst.append((cx.bank(dc), h_all[:, sch, dc * 128:(dc + 1) * 128], dt3[:, sc, :],
                                    sch == 0, sch == 31))
                cx.mms(lst, [dk] + (hkeys if sg == 0 else []), allb)
            for dc in range(8):
                eng = "act" if ne % 2 == 0 else "dve"
                ne += 1
                cx.copy(eng, FT[cs][:, dc, :], cx.bank(dc), [("ps", dc)], [("FT", cs, dc)])
        for g in range(8):
            cx.mms([(cx.bank(g), c128[:, 0, :], FT[0][:, g, :], True, False),
                    (cx.bank(g), c128[:, 1, :], FT[1][:, g, :], False, True)],
                   ["c128", ("FT", 0, g), ("FT", 1, g)], [("ps", g)])
            eng = "act" if g % 2 == 0 else "dve"
            cx.copy(eng, GTb[:, g, :], cx.bank(g), [("ps", g)], [("GT", g)])
        lst = []
        for g in range(8):
            for tl in range(4):
                for nh in range(2):
                    lst.append((cx.bank(tl * 2 + nh), GTb[:, g, tl * 128:(tl + 1) * 128],
                                wo[:, g, nh * 512:(nh + 1) * 512], g == 0, g == 7))
        cx.mms(lst, [("GT", g) for g in range(8)] + [("wo", g) for g in range(8)], allb)
        items = []
        for tl in range(4):
            r0 = kc * 512 + tl * 128
            items.append(dict(y=cx.ps[:, tl * 1024:(tl + 1) * 1024], ykeys=[("ps", 2 * tl), ("ps", 2 * tl + 1)],
                              src=rows_of(x_own, r0, 128), out=x_out(r0),
                              halo_out=halo_out if r0 == T - 128 else None))
        B.epilogue_group(items)
    cx.P.barrier()
    cx.release(m0)


def stage_attn(cx, io, consts, x_own, x_par, x_out, halo_out=None):
    m0 = cx.mark()
    ident = consts["ident"]
    bc = {}
    B = Blocks(cx, bc, n_xt=3, n_nt=2, n_tt=2)
    hb = Ring("hb", [cx.bf(1024) for _ in range(2)])
    hT = cx.bf(8 * S).rearrange("p (a b) -> p a b", b=S)
    oT = cx.bf(8 * T).rearrange("p (a b) -> p a b", b=T)
    mA = cx.mark()
    bc["sh"] = cx.f32(1024)
    bc["sc1"] = cx.f32(1024)
    wring = Ring("adaw", [cx.bf(8 * 512) for _ in range(2)])
    bring = Ring("adab", [cx.f32(512) for _ in range(2)])
    ada_vectors(cx, io, consts, 1, [(0, bc["sh"], "sh", False), (1, bc["sc1"], "sc1", True)], wring, bring, [6, 7])

    tb = [6, 7]
    ptiles = []
    for et in range(32):
        stt_ = {}

        def s0(et=et, stt_=stt_):
            src = rows_of(x_own, et * 128, 128) if et < NT else rows_of(x_par, (et - NT) * 128, 128)
            stt_["x"] = B.load_x(src)
            stt_["st"] = B.pro_a(stt_["x"][0], stt_["x"][1], 128)

        def s1(et=et, stt_=stt_):
            stt_["n"] = B.pro_b1(stt_["x"][0], stt_["x"][1], 128, stt_["st"])

        def s1b(et=et, stt_=stt_):
            stt_["h"] = hb.next()
            B.pro_b2(stt_["n"], 128, stt_["h"][0], [stt_["h"][1]])

        def s2(et=et, stt_=stt_):
            h_, hk = stt_["h"]
            b = tb[et % 2]
            pTt = cx.bank(b).bitcast(BF16)[:, 0:1024].rearrange("p (a b) -> p a b", b=128)
            cx.transposes([(pTt[:, dc, :], h_[:, dc * 128:(dc + 1) * 128]) for dc in range(8)], ident,
                          [hk, "ident"], [("ps", b)])
            eng = "act"
            cx.copy(eng, hT[:, :, et * 128:(et + 1) * 128], pTt, [("ps", b)], [("hT", et // 4)])

        ptiles.append([s0, s1, s1b, (lambda: None), s2])
    emit_skewed(ptiles)
    hkeys = [("hT", j) for j in range(8)]
    cx.P.barrier()
    cx.release(mA)
    kT = [cx.bf(S) for _ in range(2)]
    qTb = [cx.bf(T) for _ in range(2)]
    qTa = [cx.bf(T) for _ in range(2)]
    V = cx.bf(32 * 129).rearrange("p (a b) -> p a b", b=129)
    wqkv = Ring("wqkv", [cx.bf(3 * 8 * 128) for _ in range(2)])
    pT = Ring("pT", [cx.bf(512) for _ in range(4)])
    dbias = cx.bf(8 * 128).rearrange("p (a b) -> p a b", b=128)
    lamt = cx.f32(4 * 64).rearrange("p (a b) -> p a b", b=64)
    lsum = cx.f32(2)
    lexp = cx.f32(2)
    nlam = cx.f32(1)
    gs = cx.f32(128)
    fin = [{"rl": cx.f32(2), "nl1": cx.f32(1), "o0": cx.f32(128), "o": cx.f32(128), "junk": cx.f32(128),
            "ss": cx.f32(1), "rms": cx.f32(1), "on": cx.bf(128), "key": ("fin", i)} for i in range(4)]
    accS = cx.f32(3 * 387).rearrange("p (a b) -> p a b", b=387)
    cx.dma("sp", dbias, io["dbias"], (), ["dbias"])
    cx.dma("sp", lamt, io["lams"].partition_broadcast(128), (), ["lamt"])
    cx.tt("dve", lamt[:, 0, :], lamt[:, 0, :], lamt[:, 1, :], ALU.mult, ["lamt"], ["lamt"])
    cx.tt("dve", lamt[:, 2, :], lamt[:, 2, :], lamt[:, 3, :], ALU.mult, ["lamt"], ["lamt"])
    cx.P.add("dve", lambda e: e.reduce_sum(out=lsum[:, 0:1], in_=lamt[:, 0, :], axis=AX.X), ["lamt"], ["lsum0"])
    cx.P.add("dve", lambda e: e.reduce_sum(out=lsum[:, 1:2], in_=lamt[:, 2, :], axis=AX.X), ["lamt"], ["lsum1"])
    cx.act(lexp, lsum, AF.Exp, ["lsum0", "lsum1"], ["lexp"])
    cx.stt("dve", nlam, lexp[:, 1:2], -LAM_INIT, lexp[:, 0:1], ALU.add, ALU.subtract, ["lexp"], ["nlam"])
    cx.dma("sp", gs, io["subln_g"].partition_broadcast(128), (), ["gs"])
    cx.ts("dve", gs, gs, 1.0 - LAM_INIT, None, ALU.mult, None, ["gs"], ["gs"])
    for c in range(2):
        cx.dma("sp", qTb[c][64:68, :], io["qaug"][0], (), [("qTb", c, "aug")])
        cx.dma("sp", qTa[c][64:68, :], io["qaug"][1], (), [("qTa", c, "aug")])
    cx.memset("pool", V[:, :, 128:129], 1.0, [("V", "ones")])

    w_in = io["attn_w_in"]
    kaug = io["kaug"]
    def acc_ap(c, qt):
        j = qt * 2 + c
        b = 3 + j // 3
        o = 512 * b + 129 * (j % 3)
        return cx.ps[:, o:o + 129], ("ps", b)

    nev = 0
    nfin = 0
    deferred = []
    for h in range(8):
        wt, wk = wqkv.next()
        cx.dma("pool", wt, w_in[h], (), [wk])
        w4 = wt.rearrange("p (t a b) -> p t a b", t=3, a=8)
        for c in range(2):
            cx.dma("sp", kT[c][64:68, :], kaug[h], (), [("kT", c, "aug")])
        pj = 0
        for tc in range(8):
            b = 6 + pj % 2
            pj += 1
            cx.mms([(cx.bank(b), w4[:, 1, dc, :], hT[:, dc, tc * 512:(tc + 1) * 512], dc == 0, dc == 7)
                    for dc in range(8)], [wk, ("hT", tc)], [("ps", b)])
            eng = "act" if nev % 2 == 0 else "dve"
            nev += 1
            for c in range(2):
                cx.copy(eng, kT[c][0:64, tc * 512:(tc + 1) * 512], cx.bank(b)[c * 64:(c + 1) * 64, :], [("ps", b)],
                        [("kT", c, tc)])
        for tc in range(4):
            b = 6 + pj % 2
            pj += 1
            cx.mms([(cx.bank(b), w4[:, 0, dc, :], hT[:, dc, tc * 512:(tc + 1) * 512], dc == 0, dc == 7)
                    for dc in range(8)], [wk, ("hT", tc)], [("ps", b)])
            for c in range(2):
                cx.act(qTb[c][0:64, tc * 512:(tc + 1) * 512], cx.bank(b)[c * 64:(c + 1) * 64, :], AF.Copy,
                       [("ps", b)], [("qTb", c, tc)], scale=0.125)
                cx.copy("pool", qTa[c][0:64, tc * 512:(tc + 1) * 512], qTb[c][0:64, tc * 512:(tc + 1) * 512],
                        [("qTb", c, tc)], [("qTa", c, tc)])
        for t4 in range(8):
            b = 6 + pj % 2
            pj += 1
            lst = []
            for tl in range(4):
                et = t4 * 4 + tl
                for dc in range(8):
                    lst.append((cx.bank(b)[:, tl * 128:(tl + 1) * 128], hT[:, dc, et * 128:(et + 1) * 128],
                                w4[:, 2, dc, :], dc == 0, dc == 7))
            cx.mms(lst, [wk, ("hT", t4)], [("ps", b)])
            eng = "act" if nev % 2 == 0 else "dve"
            nev += 1
            cx.copy(eng, V[:, t4 * 4:(t4 + 1) * 4, 0:128], cx.bank(b).rearrange("p (a b) -> p a b", b=128),
                    [("ps", b)], [("V", t4)])
        kkeys = {c: [("kT", c, "aug")] + [("kT", c, tc) for tc in range(8)] for c in range(2)}
        for qc in range(4):
            q0 = qc * 512
            pend = []
            tiles = [(kt, c) for kt in range(32) for c in range(2)]

            def emit_score(idx, kt, c):
                sb = idx % 3
                sbank = cx.bank(sb)
                ks = kT[c][:, kt * 128:(kt + 1) * 128]
                rd = kkeys[c] + [("qTb", c, qc), ("qTa", c, qc), ("qTb", c, "aug"), ("qTa", c, "aug")]
                if kt >= 16 or kt > 4 * qc + 3:
                    lst = [(sbank, ks[0:68, :], qTa[c][0:68, q0:q0 + 512], True, True)]
                elif kt < 4 * qc:
                    lst = [(sbank, ks[0:68, :], qTb[c][0:68, q0:q0 + 512], True, True)]
                else:
                    j = kt - 4 * qc
                    lst = []
                    if j > 0:
                        lst.append((sbank[:, 0:128 * j], ks[0:68, :], qTa[c][0:68, q0:q0 + 128 * j], True, True))
                    lst.append((sbank[:, 128 * j:128 * (j + 1)], ks[0:64, :],
                                qTb[c][0:64, q0 + 128 * j:q0 + 128 * (j + 1)], True, False))
                    lst.append((sbank[:, 128 * j:128 * (j + 1)], ident, dbias[:, h, :], False, True))
                    if j < 3:
                        lst.append((sbank[:, 128 * (j + 1):512], ks[0:68, :],
                                    qTb[c][0:68, q0 + 128 * (j + 1):q0 + 512], True, True))
                    rd = rd + ["dbias", "ident"]
                cx.mms(lst, rd, [("ps", sb)])
                return sb

            def emit_rest(idx, kt, c, sb):
                p_, pk = pT.next()
                cx.act(p_, cx.bank(sb), AF.Exp, [("ps", sb)], [pk])
                lst = []
                wk_ = set()
                for ql in range(4):
                    a, akey = acc_ap(c, ql)
                    wk_.add(akey)
                    lst.append((a, p_[:, ql * 128:(ql + 1) * 128], V[:, kt, :],
                                kt == 0 and c == 0 and ql in (0, 2, 3), kt == 31, True))
                cx.mms(lst, [pk, ("V", kt // 4), ("V", "ones")], sorted(wk_))

            sbs = {}
            LOOK = 2
            for idx, (kt, c) in enumerate(tiles):
                sbs[idx] = emit_score(idx, kt, c)
                if idx >= LOOK:
                    j = idx - LOOK
                    emit_rest(j, tiles[j][0], tiles[j][1], sbs[j])
                    if j % 2 == 1 and deferred:
                        deferred.pop(0)()
            for j in range(len(tiles) - LOOK, len(tiles)):
                emit_rest(j, tiles[j][0], tiles[j][1], sbs[j])
            for bb in range(3):
                ncol = 387 if bb < 2 else 258
                cx.copy("dve", accS[:, bb, 0:ncol], cx.ps[:, 512 * (3 + bb):512 * (3 + bb) + ncol], [("ps", 3 + bb)],
                        [("accS", bb)])

            def accs_ap(c, ql):
                j = ql * 2 + c
                return accS[:, j // 3, 129 * (j % 3):129 * (j % 3) + 129], ("accS", j // 3)

            while deferred:
                deferred.pop(0)()
            chunk_thunks = []
            for ql in range(4):
                f = fin[ql]
                fk = f["key"]
                a0, k0 = accs_ap(0, ql)
                a1, k1 = accs_ap(1, ql)
                rk = sorted({k0, k1})
                qt = qc * 4 + ql

                def fa(f=f, fk=fk, a0=a0, a1=a1, rk=rk):
                    cx.P.add("dve", lambda e: e.reciprocal(out=f["rl"][:, 0:1], in_=a0[:, 128:129]), rk, [(fk, "rl0")])
                    cx.P.add("dve", lambda e: e.reciprocal(out=f["rl"][:, 1:2], in_=a1[:, 128:129]), rk, [(fk, "rl1")])
                    cx.tt("dve", f["nl1"], f["rl"][:, 1:2], nlam, ALU.mult, [(fk, "rl1"), "nlam"], [(fk, "nl1")])
                    cx.ts("dve", f["o0"], a0[:, 0:128], f["rl"][:, 0:1], None, ALU.mult, None, rk + [(fk, "rl0")],
                          [(fk, "o0")])
                    cx.stt("dve", f["o"], a1[:, 0:128], f["nl1"], f["o0"], ALU.mult, ALU.add,
                           rk + [(fk, "nl1"), (fk, "o0")], [(fk, "o")])

                def fb(f=f, fk=fk):
                    cx.act(f["junk"], f["o"], AF.Square, [(fk, "o")], [(fk, "junk"), (fk, "ss")], accum_out=f["ss"])
                    cx.act(f["rms"], f["ss"], AF.Ln, [(fk, "ss"), "epsc"], [(fk, "rms")], bias=cx.eps, scale=1.0 / 128.0)
                    cx.act(f["rms"], f["rms"], AF.Exp, [(fk, "rms")], [(fk, "rms")], scale=-0.5)

                def fc(f=f, fk=fk, h=h, qt=qt):
                    cx.stt("dve", f["on"], f["o"], f["rms"], gs, ALU.mult, ALU.mult, [(fk, "o"), (fk, "rms"), "gs"],
                           [(fk, "on")])
                    b = 6 + qt % 2
                    pTt = cx.bank(b).bitcast(BF16)[:, 0:128]
                    cx.transposes([(pTt, f["on"])], ident, [(fk, "on"), "ident"], [("ps", b)])
                    cx.copy("dve", oT[:, h, qt * 128:(qt + 1) * 128], pTt, [("ps", b)], [("oT", h, qt)])

                chunk_thunks.append((fa, fb, fc))
            A_, B_, C_ = zip(*chunk_thunks)
            deferred.extend([A_[0], A_[1], B_[0], A_[2], B_[1], C_[0], A_[3], B_[2], C_[1], B_[3], C_[2], C_[3]])
    while deferred:
        deferred.pop(0)()

    cx.P.barrier()
    cx.release(mA)
    bc["g"] = cx.f32(1024)
    bc["gam"] = cx.f32(1024)
    bc["bet"] = cx.f32(1024)
    wo = cx.bf(8 * 1024).rearrange("p (a b) -> p a b", b=1024)
    wring = Ring("adaw2", [cx.bf(8 * 512) for _ in range(2)])
    bring = Ring("adab2", [cx.f32(512) for _ in range(2)])
    ada_vectors(cx, io, consts, 1, [(2, bc["g"], "g", False)], wring, bring, [6, 7])
    bc_load(cx, bc["gam"], "gam", io["ln_mix_g1"])
    bc_load(cx, bc["bet"], "bet", io["ln_mix_b1"])
    for q in range(0, 8, 2):
        cx.dma("pool", wo[:, q:q + 2, :], io["attn_w_out"][:, q:q + 2, :], (), [("wo", q), ("wo", q + 1)])
    otiles = []
    for qt in range(NT):
        stt_ = {}

        def o0(qt=qt, stt_=stt_):
            b0 = (qt % 2) * 2
            lst = []
            for h in range(8):
                for nh in range(2):
                    lst.append((cx.bank(b0 + nh), oT[:, h, qt * 128:(qt + 1) * 128], wo[:, h, nh * 512:(nh + 1) * 512],
                                h == 0, h == 7))
            stt_["x"] = B.load_x(rows_of(x_own, qt * 128, 128))
            cx.mms(lst, [("oT", h, qt) for h in range(8)] + [("wo", g) for g in range(8)], [("ps", b0), ("ps", b0 + 1)])
            stt_["t"] = B.epi_a(cx.ps[:, b0 * 512:(b0 + 2) * 512], [("ps", b0), ("ps", b0 + 1)])

        def o1(qt=qt, stt_=stt_):
            stt_["st"] = B.epi_b1(stt_["t"][0], stt_["t"][1], stt_["x"][0], stt_["x"][1])

        def o2(qt=qt, stt_=stt_):
            B.epi_b2(stt_["t"][0], stt_["t"][1], stt_["st"], x_out(qt * 128), halo_out if qt == NT - 1 else None)

        otiles.append([o0, o1, o2])
    emit_skewed(otiles)
    cx.P.barrier()
    cx.release(m0)


ARENA_BYTES = 206 * 1024

IN_SPECS = {
    "ident": ([128, 128], BF16),
    "c_t": ([128, 8], F32),
    "ada_w0": ([12, 128, 4096], F32), "ada_b0": ([6 * D], F32),
    "ada_w1": ([12, 128, 4096], F32), "ada_b1": ([6 * D], F32),
    "ln_mix_g0": ([D], F32), "ln_mix_b0": ([D], F32), "ln_ffn_g0": ([D], F32), "ln_ffn_b0": ([D], F32),
    "ln_mix_g1": ([D], F32), "ln_mix_b1": ([D], F32), "ln_ffn_g1": ([D], F32), "ln_ffn_b1": ([D], F32),
    "fnet_w_out": ([128, 8, D], F32), "c128": ([128, 2, 128], BF16), "dftc": ([4, 8, 128, 2048], BF16), "dfts": ([4, 8, 128, 2048], BF16),
    "w_up0": ([NPAIR, 128, 2048], F32), "cwb0": ([128, 44, 4], F32), "w_down0": ([128, NPAIR, D], F32),
    "w_up1": ([NPAIR, 128, 2048], F32), "cwb1": ([128, 44, 4], F32), "w_down1": ([128, NPAIR, D], F32),
    "attn_w_in": ([8, 128, 3072], F32), "attn_w_out": ([128, 8, D], F32), "lams": ([4, 64], F32),
    "subln_g": ([128], F32), "kaug": ([8, 4, S], BF16), "qaug": ([2, 4, T], BF16), "dbias": ([128, 8, 128], BF16),
}

STAGE_INPUTS = {
    "fnet": ["ident", "c_t", "ada_w0", "ada_b0", "ln_mix_g0", "ln_mix_b0", "fnet_w_out", "c128", "dftc", "dfts"],
    "ffn0": ["ident", "c_t", "ada_w0", "ada_b0", "ln_ffn_g0", "ln_ffn_b0", "w_up0", "cwb0", "w_down0"],
    "attn": ["ident", "c_t", "ada_w1", "ada_b1", "ln_mix_g1", "ln_mix_b1", "attn_w_in", "attn_w_out", "lams",
             "subln_g", "kaug", "qaug", "dbias"],
    "ffn1": ["ident", "c_t", "ada_w1", "ada_b1", "ln_ffn_g1", "ln_ffn_b1", "w_up1", "cwb1", "w_down1"],
}


def build_stage_program(stage):
    nc = bass.Bass("TRN2", target_bir_lowering=False)
    io = {}
    for name in STAGE_INPUTS[stage]:
        shape, dt = IN_SPECS[name]
        io[name] = nc.dram_tensor(name, shape, dt, kind="ExternalInput").ap()
    x_own = nc.dram_tensor("x_own", [T, D], F32, kind="ExternalInput").ap()
    if stage in ("fnet", "attn"):
        x_par = nc.dram_tensor("x_par", [T, D], F32, kind="ExternalInput").ap()
    else:
        halo = nc.dram_tensor("halo", [1, D], F32, kind="ExternalInput").ap()
    x_out = nc.dram_tensor("x_out", [T, D], F32, kind="ExternalOutput").ap()
    with contextlib.ExitStack() as es:
        arena = es.enter_context(nc.sbuf_tensor("arena", [128, ARENA_BYTES], U8))
        ps = es.enter_context(nc.psum_tensor("ps", [128, 4096], F32))
        cx = Cx(nc, arena, ARENA_BYTES, ps)
        consts = load_consts(cx, io)
        ada_setup(cx, io, consts)
        xo = lambda r0: (x_out[r0:r0 + 128, :], ("xout", r0))
        if stage == "fnet":
            stage_fnet(cx, io, consts, ("d", x_own), ("d", x_par), xo)
        elif stage == "attn":
            stage_attn(cx, io, consts, ("d", x_own), ("d", x_par), xo)
        else:
            stage_ffn(cx, io, consts, int(stage[-1]), ("d", x_own), ("d", halo), xo)
        cx.P.build()
    return nc


_CONST_CACHE = {}


def _dft_consts(parity):
    key = ("dft", parity)
    if key in _CONST_CACHE:
        return _CONST_CACHE[key]
    e = np.arange(S)
    pos = np.where(e < T, e, 6143 - e)
    k = np.arange(T)
    if parity == 0:
        gp, gk = pos, k
    else:
        gp, gk = 4095 - pos, 4095 - k
    prod = (gp[:, None].astype(np.int64) * gk[None, :].astype(np.int64)) % S
    ang = (2.0 * np.pi / S) * prod.astype(np.float64)
    out = []
    for fn in (np.cos, np.sin):
        m = (fn(ang) / 64.0).astype(np.float32)
        m = m.reshape(8, 4, 128, 4, 512)
        out.append(np.ascontiguousarray(m.transpose(3, 0, 2, 1, 4).astype(NPBF)).reshape(4, 8, 128, 2048))
    _CONST_CACHE[key] = out
    return out


def _static_consts():
    if "static" in _CONST_CACHE:
        return _CONST_CACHE["static"]
    c = {}
    c["ident"] = np.eye(128, dtype=np.float32).astype(NPBF)
    dd = np.arange(128)
    ang = 2.0 * np.pi * ((dd[:, None] * dd[None, :]) % 128) / 128.0
    sc = 1.0 / math.sqrt(128.0)
    c128 = np.stack([np.cos(ang) * sc, -np.sin(ang) * sc], axis=1)
    c["c128"] = c128.astype(np.float32).astype(NPBF)
    slopes = np.array([2.0 ** (-(h + 1)) for h in range(8)], dtype=np.float64)
    e = np.arange(S)
    pos = np.where(e < T, e, 6143 - e)
    k0 = (pos // 128) * 128
    kl = pos - k0
    kaug = np.empty((8, 4, S), dtype=np.float64)
    for h in range(8):
        kaug[h, 0] = slopes[h] * k0
        kaug[h, 1] = slopes[h] * kl
        kaug[h, 2] = -slopes[h]
        kaug[h, 3] = -slopes[h]
    c["kaug"] = kaug.astype(np.float32).astype(NPBF)
    j = np.arange(T)
    q0 = (j // 256) * 256
    ql = j - q0
    qb = np.stack([np.ones(T), np.ones(T), q0, ql]).astype(np.float64)
    c["qaug"] = np.stack([qb, -qb]).astype(np.float32).astype(NPBF)
    dist = np.abs(dd[:, None] - dd[None, :]).astype(np.float64)
    db = np.stack([-slopes[h] * dist for h in range(8)], axis=1)
    c["dbias"] = db.astype(np.float32).astype(NPBF)
    _CONST_CACHE["static"] = c
    return c


def _prep_weights(inp):
    L = [
        dict(ada_w=inp["l0_ada_w"], ada_b=inp["l0_ada_b"], ln_mix_g=inp["l0_ln_mix_g"], ln_mix_b=inp["l0_ln_mix_b"],
             ffn_w_up=inp["l0_ffn_w_up"], ffn_conv_w=inp["l0_ffn_conv_w"], ffn_conv_b=inp["l0_ffn_conv_b"],
             ffn_w_down=inp["l0_ffn_w_down"], ln_ffn_g=inp["l0_ln_ffn_g"], ln_ffn_b=inp["l0_ln_ffn_b"]),
        dict(ada_w=inp["l1_ada_w"], ada_b=inp["l1_ada_b"], ln_mix_g=inp["l1_ln_mix_g"], ln_mix_b=inp["l1_ln_mix_b"],
             ffn_w_up=inp["l1_ffn_w_up"], ffn_conv_w=inp["l1_ffn_conv_w"], ffn_conv_b=inp["l1_ffn_conv_b"],
             ffn_w_down=inp["l1_ffn_w_down"], ln_ffn_g=inp["l1_ln_ffn_g"], ln_ffn_b=inp["l1_ln_ffn_b"]),
    ]
    w = {}
    for i in range(2):
        P = L[i]
        w["ada_w%d" % i] = np.ascontiguousarray(P["ada_w"].reshape(8, 128, 12, 512).transpose(2, 1, 0, 3)).reshape(12, 128, 4096)
        w["ada_b%d" % i] = np.ascontiguousarray(P["ada_b"])
        for n in ("ln_mix_g", "ln_mix_b", "ln_ffn_g", "ln_ffn_b"):
            w["%s%d" % (n, i)] = np.ascontiguousarray(P[n])
        wu = P["ffn_w_up"].reshape(8, 128, 2, NPAIR, 128)
        w["w_up%d" % i] = np.ascontiguousarray(wu.transpose(3, 1, 2, 0, 4)).reshape(NPAIR, 128, 2048)
        wdn = P["ffn_w_down"].reshape(NPAIR, 128, D)
        w["w_down%d" % i] = np.ascontiguousarray(wdn.transpose(1, 0, 2))
        cw = P["ffn_conv_w"]
        cb = P["ffn_conv_b"]
        for par in range(2):
            taps = cw if par == 0 else cw[::-1]
            t = np.concatenate([taps, cb[None, :]], axis=0)
            t = t.reshape(4, 44, 128).transpose(2, 1, 0)
            w["cwb%d_%d" % (i, par)] = np.ascontiguousarray(t)
    w["fnet_w_out"] = np.ascontiguousarray(inp["l0_fnet_w_out"].reshape(8, 128, D).transpose(1, 0, 2))
    w["attn_w_out"] = np.ascontiguousarray(inp["l1_attn_w_out"].reshape(8, 128, D).transpose(1, 0, 2))
    wi = inp["l1_attn_w_in"].reshape(8, 128, 3, 8, 128)
    w["attn_w_in"] = np.ascontiguousarray(wi.transpose(3, 1, 2, 0, 4)).reshape(8, 128, 3072)
    w["lams"] = np.ascontiguousarray(np.stack([inp["l1_attn_lambda_q1"], inp["l1_attn_lambda_k1"],
                                               inp["l1_attn_lambda_q2"], inp["l1_attn_lambda_k2"]]))
    w["subln_g"] = np.ascontiguousarray(inp["l1_attn_subln_g"])
    return w


def _core_inputs(stage, r, w, cst, c):
    b, par = r // 2, r % 2
    m = {}
    for name in (STAGE_INPUTS[stage] if stage is not None else list(IN_SPECS)):
        if name == "c_t":
            m[name] = np.ascontiguousarray(c[b].reshape(8, 128).T)
        elif name.startswith("cwb"):
            m[name] = w["%s_%d" % (name, par)]
        elif name == "dftc":
            m[name] = _dft_consts(par)[0]
        elif name == "dfts":
            m[name] = _dft_consts(par)[1]
        elif name in cst:
            m[name] = cst[name]
        else:
            m[name] = w[name]
    return m


def _to_local(xfull, r):
    b, par = r // 2, r % 2
    xb = xfull[b]
    if par == 0:
        own, partner = xb[:T], xb[::-1][:T]
    else:
        own, partner = xb[::-1][:T], xb[:T]
    return np.ascontiguousarray(own), np.ascontiguousarray(partner)


def _from_local(outs):
    full = np.empty((4, S, D), dtype=np.float32)
    for r, o in enumerate(outs):
        b, par = r // 2, r % 2
        if par == 0:
            full[b, :T] = o
        else:
            full[b, T:] = o[::-1]
    return full


_PROGS = {}


def _run_stage(stage, xfull, w, cst, c):
    if stage not in _PROGS:
        _PROGS[stage] = build_stage_program(stage)
    nc = _PROGS[stage]
    in_maps = []
    for r in range(8):
        m = _core_inputs(stage, r, w, cst, c)
        own, partner = _to_local(xfull, r)
        m["x_own"] = own
        if stage in ("fnet", "attn"):
            m["x_par"] = partner
        else:
            m["halo"] = np.ascontiguousarray(partner[T - 1:T])
        in_maps.append(m)
    res = run_bass_kernel_spmd(nc, in_maps, core_ids=list(range(8)))
    return _from_local([np.asarray(res.results[r]["x_out"]) for r in range(8)])


RG = [[0, 1], [2, 3], [4, 5], [6, 7]]


def build_fused_program():
    nc = bass.Bass("TRN2", target_bir_lowering=False)
    io = {}
    for name, (shape, dt) in IN_SPECS.items():
        io[name] = nc.dram_tensor(name, shape, dt, kind="ExternalInput").ap()
    x_own = nc.dram_tensor("x_own", [T, D], F32, kind="ExternalInput").ap()
    x_par = nc.dram_tensor("x_par", [T, D], F32, kind="ExternalInput").ap()
    mask_in = nc.dram_tensor("mask", [128, 2], F32, kind="ExternalInput").ap()
    x_out = nc.dram_tensor("x_out", [T, D], F32, kind="ExternalOutput").ap()
    t_xs1 = nc.dram_tensor("xs1", [T, D], F32)
    t_hal1 = nc.dram_tensor("hal1", [1, D], F32)
    t_hg1 = nc.dram_tensor("hg1", [2, D], F32)
    t_xs2 = [nc.dram_tensor("xs2_%d" % i, [128, D], F32) for i in range(NT)]
    t_xg2 = [nc.dram_tensor("xg2_%d" % i, [256, D], F32) for i in range(NT)]
    t_xs3 = nc.dram_tensor("xs3", [T, D], F32)
    t_hal3 = nc.dram_tensor("hal3", [1, D], F32)
    t_hg3 = nc.dram_tensor("hg3", [2, D], F32)
    xs1, hal1, hg1, xs3, hal3, hg3 = (t.ap() for t in (t_xs1, t_hal1, t_hg1, t_xs3, t_hal3, t_hg3))
    xs2 = [t.ap() for t in t_xs2]
    xg2 = [t.ap() for t in t_xg2]

    def gather(cx, tin, tout, reads, writes):
        cx.P.add("pool", lambda e: e.collective_compute("AllGather", ALU.bypass, replica_groups=RG,
                                                        ins=[tin.ap().opt()], outs=[tout.ap().opt()]),
                 reads, writes, cc=True)

    with contextlib.ExitStack() as es:
        arena = es.enter_context(nc.sbuf_tensor("arena", [128, ARENA_BYTES], U8))
        ps = es.enter_context(nc.psum_tensor("ps", [128, 4096], F32))
        cx = Cx(nc, arena, ARENA_BYTES, ps)
        consts = load_consts(cx, io)
        ada_setup(cx, io, consts)
        mask = cx.f32(2)
        cx.dma("sp", mask, mask_in, (), ["mask"])

        stage_fnet(cx, io, consts, ("d", x_own), ("d", x_par),
                   lambda r0: (xs1[r0:r0 + 128, :], ("xs1", r0)), halo_out=hal1)
        gather(cx, t_hal1, t_hg1, ["halo_out"], ["hg1"])
        x_own2 = ("f", lambda r0, n: ("d", xs1[r0:r0 + n, :], [("xs1", r0)]))
        halo2 = ("m", hg1[0:1, :], hg1[1:2, :], mask, ["hg1"])

        def xo2(r0):
            return xs2[r0 // 128], ("xs2", r0)

        pend = []

        def hook2(r0):
            pend.append(r0)
            if len(pend) > 1:
                p = pend.pop(0)
                gather(cx, t_xs2[p // 128], t_xg2[p // 128], [("xs2", p)], [("xg2", p)])

        stage_ffn(cx, io, consts, 0, x_own2, halo2, xo2, tile_hook=hook2)
        for p in pend:
            gather(cx, t_xs2[p // 128], t_xg2[p // 128], [("xs2", p)], [("xg2", p)])

        def own3(r0, n):
            return ("d", xs2[r0 // 128], [("xs2", r0)])

        def par3(r0, n):
            g = xg2[r0 // 128]
            return ("m", g[0:128, :], g[128:256, :], mask, [("xg2", r0)])

        stage_attn(cx, io, consts, ("f", own3), ("f", par3),
                   lambda r0: (xs3[r0:r0 + 128, :], ("xs3", r0)), halo_out=hal3)
        gather(cx, t_hal3, t_hg3, ["halo_out"], ["hg3"])
        x_own4 = ("f", lambda r0, n: ("d", xs3[r0:r0 + n, :], [("xs3", r0)]))
        halo4 = ("m", hg3[0:1, :], hg3[1:2, :], mask, ["hg3"])
        stage_ffn(cx, io, consts, 1, x_own4, halo4, lambda r0: (x_out[r0:r0 + 128, :], ("xout", r0)))
        cx.P.build()
    return nc


_FUSED = []


def kernel(**inputs):
    inp = {k: np.asarray(v) for k, v in inputs.items()}
    w = _prep_weights(inp)
    cst = _static_consts()
    x = np.ascontiguousarray(inp["x"], dtype=np.float32)
    c = inp["c"]
    if not _FUSED:
        _FUSED.append(build_fused_program())
    nc = _FUSED[0]
    in_maps = []
    for r in range(8):
        m = _core_inputs(None, r, w, cst, c)
        own, partner = _to_local(x, r)
        m["x_own"] = own
        m["x_par"] = partner
        mk = np.zeros((128, 2), np.float32)
        mk[:, 1 - (r % 2)] = 1.0
        m["mask"] = mk
        in_maps.append(m)
    res = run_bass_kernel_spmd(nc, in_maps, core_ids=list(range(8)))
    return _from_local([np.asarray(res.results[r]["x_out"]) for r in range(8)])
```

```python
import contextlib
import math

import ml_dtypes
import numpy as np

import concourse.bass as bass
import concourse.mybir as mybir
from concourse.bass_utils import run_bass_kernel_spmd

F32 = mybir.dt.float32
BF16 = mybir.dt.bfloat16
U8 = mybir.dt.uint8
ALU = mybir.AluOpType
AF = mybir.ActivationFunctionType
AX = mybir.AxisListType
NPBF = ml_dtypes.bfloat16

D = 1024
S = 4096
T = 2048
NT = 16
DFF = 2816
NPAIR = 22
ALPHA = 4.0 ** 0.25
EPS = 1e-5
LAM_INIT = 0.8 - 0.6 * math.exp(-0.3 * 1)
ENGS = ("pe", "act", "dve", "pool", "sp")


class _Op:
    __slots__ = ("eng", "emit", "deps", "idx", "signal", "dma", "sem", "val", "prev", "cc")

    def __init__(self, eng, emit, dma):
        self.eng = eng
        self.emit = emit
        self.deps = set()
        self.signal = False
        self.dma = dma
        self.cc = False
        self.sem = None
        self.val = 0
        self.prev = 0


class Prog:
    def __init__(self, nc, n_dma_sems=14):
        self.nc = nc
        self.ops = []
        self.last_w = {}
        self.readers = {}
        self.n_dma_sems = n_dma_sems
        self.since_barrier = []

    def add(self, eng, emit, reads=(), writes=(), dma=False, cc=False):
        op = _Op(eng, emit, dma or cc)
        op.cc = cc
        op.idx = len(self.ops)
        ops = self.ops
        for r in reads:
            w = self.last_w.get(r)
            if w is not None:
                op.deps.add(w)
            if isinstance(r, tuple) and r[0] == "ps":
                for pr in self.readers.get(r, ()):
                    if ops[pr].eng != eng:
                        op.deps.add(pr)
        for wkey in writes:
            w = self.last_w.get(wkey)
            if w is not None:
                wo = ops[w]
                if wo.dma or dma or wo.eng != eng:
                    op.deps.add(w)
            for r in self.readers.get(wkey, ()):
                ro = ops[r]
                if ro.dma or dma or ro.eng != eng:
                    op.deps.add(r)
        for r in reads:
            self.readers.setdefault(r, []).append(op.idx)
        for wkey in writes:
            self.last_w[wkey] = op.idx
            self.readers[wkey] = []
        op.deps.discard(op.idx)
        ops.append(op)
        self.since_barrier.append(op.idx)
        return op.idx

    def barrier(self):
        prev = [i for i in self.since_barrier if not self.ops[i].cc]
        self.since_barrier = []
        for e in ENGS:
            op = _Op(e, None, False)
            op.idx = len(self.ops)
            op.deps = set(prev)
            self.ops.append(op)

    def build(self, final_wait_eng="sp"):
        nc = self.nc
        ops = self.ops
        for op in ops:
            for d in op.deps:
                ops[d].signal = True
        EPOCH = 384
        n_sig = {e: 0 for e in ENGS}
        n_dma = {e: 0 for e in ("sp", "act", "pool")}
        for op in ops:
            if op.emit is None:
                continue
            if op.cc:
                pass
            elif op.dma:
                n_dma[op.eng] += 1
            elif op.signal:
                n_sig[op.eng] += 1
        with contextlib.ExitStack() as es:
            esem = {e: [es.enter_context(nc.semaphore("s_%s%d" % (e, i)))
                        for i in range((n_sig[e] + EPOCH - 1) // EPOCH)] for e in ENGS}
            npool = {e: (max(self.n_dma_sems, (n_dma[e] * 16 + EPOCH - 1) // EPOCH) if n_dma[e] else 0) for e in n_dma}
            dsem = {e: [es.enter_context(nc.semaphore("d_%s%d" % (e, i))) for i in range(npool[e])] for e in n_dma}
            ticks = {e: 0 for e in ENGS}
            dcnt = {e: 0 for e in dsem}
            dval = {e: [0] * npool[e] for e in dsem}
            for op in ops:
                if op.emit is None:
                    continue
                if op.cc:
                    op.sem = es.enter_context(nc.semaphore("cc%d" % op.idx))
                    op.prev = 0
                    op.val = 1
                elif op.dma:
                    i = dcnt[op.eng] % npool[op.eng]
                    dcnt[op.eng] += 1
                    op.sem = dsem[op.eng][i]
                    op.prev = dval[op.eng][i]
                    dval[op.eng][i] += 16
                    op.val = dval[op.eng][i]
                elif op.signal:
                    t = ticks[op.eng]
                    ticks[op.eng] += 1
                    op.sem = esem[op.eng][t // EPOCH]
                    op.val = t % EPOCH + 1
            per_eng = {e: [op for op in ops if op.eng == e] for e in ENGS}
            block = es.enter_context(nc.Block())

            def run(ename, eng):
                waited = {}

                def wait(sem, val):
                    k = id(sem)
                    if waited.get(k, 0) >= val:
                        return
                    waited[k] = val
                    eng.wait_ge(sem, val)

                for op in per_eng[ename]:
                    need = {}
                    for d in op.deps:
                        do = ops[d]
                        if do.sem is None:
                            continue
                        k = id(do.sem)
                        if k not in need or need[k][1] < do.val:
                            need[k] = (do.sem, do.val)
                    for sem, val in need.values():
                        wait(sem, val)
                    if op.emit is None:
                        continue
                    if op.dma and op.prev > 0:
                        wait(op.sem, op.prev)
                    ins = op.emit(eng)
                    if op.cc:
                        ins.then_inc(op.sem)
                    elif op.dma:
                        ins.then_inc(op.sem, 16)
                    elif op.signal:
                        ins.then_inc(op.sem, 1)
                if ename == final_wait_eng:
                    for e in dsem:
                        for i in range(npool[e]):
                            if dval[e][i] > 0:
                                wait(dsem[e][i], dval[e][i])

            @block.sync
            def _(eng):
                run("sp", eng)

            @block.scalar
            def _(eng):
                run("act", eng)

            @block.vector
            def _(eng):
                run("dve", eng)

            @block.gpsimd
            def _(eng):
                run("pool", eng)

            @block.tensor
            def _(eng):
                run("pe", eng)


class Ring:
    def __init__(self, name, aps, extra=()):
        self.items = [(ap, (name, j)) for j, ap in enumerate(aps)] + list(extra)
        self.i = 0

    def next(self):
        j = self.i % len(self.items)
        self.i += 1
        return self.items[j]


def emit_skewed(tiles):
    nseg = max(len(t) for t in tiles)
    for step in range(len(tiles) + nseg - 1):
        for sgi in range(nseg - 1, -1, -1):
            t = step - sgi
            if 0 <= t < len(tiles) and sgi < len(tiles[t]):
                tiles[t][sgi]()


class Cx:
    def __init__(self, nc, arena, arena_bytes, ps):
        self.nc = nc
        self.P = Prog(nc)
        self.arena = arena
        self.arena_bytes = arena_bytes
        self.top = 0
        self.peak = 0
        self.ps = ps

    def take(self, nbytes, dtype):
        off = (self.top + 63) // 64 * 64
        self.top = off + nbytes
        self.peak = max(self.peak, self.top)
        assert self.top <= self.arena_bytes, ("SBUF arena overflow", self.top)
        return self.arena[:, off:off + nbytes].bitcast(dtype)

    def f32(self, n):
        return self.take(4 * n, F32)

    def bf(self, n):
        return self.take(2 * n, BF16)

    def mark(self):
        return self.top

    def release(self, m):
        self.top = m

    def bank(self, b, nb=1):
        return self.ps[:, 512 * b:512 * (b + nb)]

    def dma(self, q, out, in_, reads=(), writes=()):
        self.P.add(q, lambda e: e.dma_start(out=out, in_=in_), reads, writes, dma=True)

    def act(self, out, in_, func, reads, writes, bias=None, scale=None, accum_out=None):
        kw = {}
        if bias is not None:
            kw["bias"] = bias
        if scale is not None:
            kw["scale"] = scale
        if accum_out is not None:
            kw["accum_out"] = accum_out
        self.P.add("act", lambda e: e.activation(out=out, in_=in_, func=func, **kw), reads, writes)

    def tt(self, eng, out, in0, in1, op, reads, writes):
        self.P.add(eng, lambda e: e.tensor_tensor(out=out, in0=in0, in1=in1, op=op), reads, writes)

    def ts(self, eng, out, in0, s1, s2, op0, op1, reads, writes):
        if s2 is None:
            self.P.add(eng, lambda e: e.tensor_scalar(out=out, in0=in0, scalar1=s1, scalar2=None, op0=op0), reads, writes)
        else:
            self.P.add(eng, lambda e: e.tensor_scalar(out=out, in0=in0, scalar1=s1, scalar2=s2, op0=op0, op1=op1), reads, writes)

    def stt(self, eng, out, in0, scalar, in1, op0, op1, reads, writes):
        self.P.add(eng, lambda e: e.scalar_tensor_tensor(out=out, in0=in0, scalar=scalar, in1=in1, op0=op0, op1=op1), reads, writes)

    def copy(self, eng, out, in_, reads, writes):
        if eng == "act":
            self.P.add("act", lambda e: e.copy(out=out, in_=in_), reads, writes)
        else:
            self.P.add(eng, lambda e: e.tensor_copy(out=out, in_=in_), reads, writes)

    def memset(self, eng, ap, val, writes):
        self.P.add(eng, lambda e: e.memset(ap, val), (), writes)

    def mms(self, lst, reads, writes):
        lst = list(lst)

        def f(e):
            ins = None
            for t in lst:
                o, l, r, st, sp = t[:5]
                if len(t) > 5 and t[5]:
                    ins = e.matmul(o, lhsT=l, rhs=r, start=st, stop=sp, skip_group_check=True)
                else:
                    ins = e.matmul(o, lhsT=l, rhs=r, start=st, stop=sp)
            return ins

        self.P.add("pe", f, reads, writes)

    def transposes(self, lst, ident, reads, writes):
        lst = list(lst)

        def f(e):
            ins = None
            for (o, i) in lst:
                ins = e.transpose(o, i, ident)
            return ins

        self.P.add("pe", f, reads, writes)


def load_consts(cx, io):
    c = {}
    c["ident"] = cx.bf(128)
    cx.dma("sp", c["ident"], io["ident"], (), ["ident"])
    c["ones"] = cx.bf(128)
    cx.memset("pool", c["ones"], 1.0, ["ones"])
    c["eps"] = cx.f32(1)
    cx.memset("pool", c["eps"], EPS, ["epsc"])
    cx.eps = c["eps"]
    return c


def ada_setup(cx, io, consts):
    cs_raw = cx.f32(8)
    cs = cx.f32(8)
    cbc = cx.bf(8 * 128).rearrange("p (a b) -> p a b", b=128)
    cx.dma("sp", cs_raw, io["c_t"], (), ["cs_raw"])
    cx.act(cs, cs_raw, AF.Silu, ["cs_raw"], ["cs"])
    for kc in range(8):
        cx.ts("dve", cbc[:, kc, :], consts["ones"], cs[:, kc:kc + 1], None, ALU.mult, None,
              ["ones", "cs"], [("cbc", kc)])
    consts["cbc"] = cbc


def ada_vectors(cx, io, consts, layer, items, wring, bring, banks):
    cbc = consts["cbc"]
    ada_w = io["ada_w%d" % layer]
    ada_b = io["ada_b%d" % layer]
    n = 0
    for (ci, dest, dkey, plus_one) in items:
        for half in range(2):
            c0 = ci * 1024 + half * 512
            wt, wk = wring.next()
            bt, bk = bring.next()
            wt3 = wt.rearrange("p (a b) -> p a b", b=512)
            cx.dma("pool", wt, ada_w[ci * 2 + half], (), [wk])
            cx.dma("sp", bt, ada_b[c0:c0 + 512].partition_broadcast(128), (), [bk])
            b = banks[n % len(banks)]
            n += 1
            pb = cx.bank(b)
            cx.mms([(pb, cbc[:, kc, :], wt3[:, kc, :], kc == 0, kc == 7) for kc in range(8)],
                   [wk] + [("cbc", kc) for kc in range(8)], [("ps", b)])
            cx.stt("dve", dest[:, half * 512:(half + 1) * 512], pb, 1.0 if plus_one else 0.0, bt,
                   ALU.add, ALU.add, [("ps", b), bk], [(dkey, half)])


def bc_load(cx, dest, dkey, vec_ap):
    cx.dma("sp", dest, vec_ap.partition_broadcast(128), (), [(dkey, 0), (dkey, 1)])


def ln_stats(cx, src, rows, skey, st, reads):
    stats, mv, rstd, nmr = st["stats"], st["mv"], st["rstd"], st["nmr"]
    k = st["key"]
    cx.P.add("dve", lambda e: e.bn_stats(out=stats[:rows, 0:6], in_=src[:rows, 0:512]), reads, [(k, "s0")])
    cx.P.add("dve", lambda e: e.bn_stats(out=stats[:rows, 6:12], in_=src[:rows, 512:1024]), reads, [(k, "s1")])
    cx.P.add("dve", lambda e: e.bn_aggr(out=mv[:rows, :], in_=stats[:rows, :]), [(k, "s0"), (k, "s1")], [(k, "mv")])
    cx.act(rstd[:rows, :], mv[:rows, 1:2], AF.Sqrt, [(k, "mv"), "epsc"], [(k, "rstd")], bias=cx.eps[:rows, :], scale=1.0)
    cx.P.add("dve", lambda e: e.reciprocal(out=rstd[:rows, :], in_=rstd[:rows, :]), [(k, "rstd")], [(k, "rstd")])
    cx.ts("dve", nmr[:rows, :], mv[:rows, 0:1], rstd[:rows, :], -1.0, ALU.mult, ALU.mult,
          [(k, "mv"), (k, "rstd")], [(k, "nmr")])


def make_stat_ring(cx, name, n):
    sts = []
    for i in range(n):
        sts.append({"stats": cx.f32(12), "mv": cx.f32(2), "rstd": cx.f32(1), "nmr": cx.f32(1), "key": (name, i)})
    return sts


class Blocks:
    def __init__(self, cx, bc, n_xt=3, n_nt=2, n_tt=2, xt_extra=(), tt_extra=()):
        self.cx = cx
        self.bc = bc
        self.xt = Ring("xt", [cx.f32(1024) for _ in range(n_xt)], xt_extra)
        self.nt = Ring("nt", [cx.f32(1024) for _ in range(n_nt)])
        self.tt_ = Ring("tt", [cx.f32(1024) for _ in range(n_tt)], tt_extra)
        self.st = make_stat_ring(cx, "st", 6)
        self.sti = 0

    def _st(self):
        s = self.st[self.sti % len(self.st)]
        self.sti += 1
        return s

    def load_x(self, src, rows=128):
        cx = self.cx
        xt, xk = self.xt.next()
        if src[0] == "d":
            rk = list(src[2]) if len(src) > 2 else []
            cx.dma("sp", xt[:rows, :], src[1], rk, [xk])
        else:
            rk = list(src[4]) if len(src) > 4 else []
            tmp, tk = self.tt_.next()
            cx.dma("sp", xt[:rows, :], src[1], rk, [xk])
            cx.dma("sp", tmp[:rows, :], src[2], rk, [tk])
            m = src[3]
            cx.act(tmp[:rows, :], tmp[:rows, :], AF.Copy, [tk, "mask"], [tk], scale=m[:rows, 1:2])
            cx.stt("dve", xt[:rows, :], xt[:rows, :], m[:rows, 0:1], tmp[:rows, :], ALU.mult, ALU.add,
                   [xk, tk, "mask"], [xk])
        return xt, xk

    def pro_a(self, xt, xk, rows):
        st = self._st()
        ln_stats(self.cx, xt, rows, None, st, [xk])
        return st

    def pro_b(self, xt, xk, rows, st, out, okeys):
        nn = self.pro_b1(xt, xk, rows, st)
        self.pro_b2(nn, rows, out, okeys)

    def pro_b1(self, xt, xk, rows, st):
        cx = self.cx
        n, nk = self.nt.next()
        k = st["key"]
        cx.act(n[:rows, :], xt[:rows, :], AF.Identity, [xk, (k, "rstd"), (k, "nmr")], [nk],
               bias=st["nmr"][:rows, :], scale=st["rstd"][:rows, :])
        return n, nk

    def pro_b2(self, nn, rows, out, okeys):
        cx = self.cx
        n, nk = nn
        cx.tt("dve", n[:rows, :], n[:rows, :], self.bc["sc1"][:rows, :], ALU.mult,
              [nk, ("sc1", 0), ("sc1", 1)], [nk])
        cx.tt("pool", out, n[:rows, :], self.bc["sh"][:rows, :], ALU.add,
              [nk, ("sh", 0), ("sh", 1)], okeys)

    def epi_a(self, y, ykeys):
        t, tk = self.tt_.next()
        self.cx.tt("dve", t, y, self.bc["g"], ALU.mult, list(ykeys) + [("g", 0), ("g", 1)], [tk])
        return t, tk

    def epi_b1(self, t, tk, xt, xk):
        cx = self.cx
        cx.stt("dve", t, xt, ALPHA, t, ALU.mult, ALU.add, [xk, tk], [tk])
        st = self._st()
        ln_stats(cx, t, 128, None, st, [tk])
        return st

    def epi_b2(self, t, tk, st, out_dram, halo_out=None):
        cx = self.cx
        k = st["key"]
        n, nk = self.nt.next()
        cx.act(n, t, AF.Identity, [tk, (k, "rstd"), (k, "nmr")], [nk], bias=st["nmr"], scale=st["rstd"])
        cx.tt("pool", n, n, self.bc["gam"], ALU.mult, [nk, ("gam", 0), ("gam", 1)], [nk])
        cx.tt("pool", n, n, self.bc["bet"], ALU.add, [nk, ("bet", 0), ("bet", 1)], [nk])
        oap, okey = out_dram
        cx.dma("sp", oap, n, [nk], [okey])
        if halo_out is not None:
            cx.dma("sp", halo_out, n[127:128, :], [nk], ["halo_out"])

    def epilogue_group(self, items):
        sts = []
        for it in items:
            it["x"] = self.load_x(it["src"])
        for it in items:
            it["t"] = self.epi_a(it["y"], it["ykeys"])
        tiles = []
        for it in items:
            def b1(it=it):
                it["st"] = self.epi_b1(it["t"][0], it["t"][1], it["x"][0], it["x"][1])

            def b2(it=it):
                self.epi_b2(it["t"][0], it["t"][1], it["st"], it["out"], it.get("halo_out"))
                if it.get("hook") is not None:
                    it["hook"]()
            tiles.append([b1, b2])
        emit_skewed(tiles)


def rows_of(src, r0, n):
    if src[0] == "f":
        return src[1](r0, n)
    if src[0] == "d":
        return ("d", src[1][r0:r0 + n, :], src[2] if len(src) > 2 else [])
    return ("m", src[1][r0:r0 + n, :], src[2][r0:r0 + n, :], src[3], src[4] if len(src) > 4 else [])


def stage_ffn(cx, io, consts, layer, x_own, halo, x_out, halo_out=None, tile_hook=None):
    m0 = cx.mark()
    bc = {k: cx.f32(1024) for k in ("sh", "sc1", "g", "gam", "bet")}
    av = Ring("av", [cx.f32(1024) for _ in range(2)])
    ag = Ring("ag", [cx.f32(1024) for _ in range(2)])
    gl = Ring("gl", [cx.f32(1024) for _ in range(1)])
    B = Blocks(cx, bc, n_xt=2, n_nt=2, n_tt=2, xt_extra=av.items, tt_extra=ag.items)
    hb = Ring("hb", [cx.bf(1024) for _ in range(2)])
    hT = cx.bf(2 * 8 * 1026).rearrange("p (h a b) -> p h a b", h=2, a=8)
    wup = Ring("wup", [cx.bf(2 * 8 * 128) for _ in range(3)])
    cwb = cx.f32(44 * 4).rearrange("p (a b) -> p a b", b=4)
    wd = cx.bf(NPAIR * 1024).rearrange("p (a b) -> p a b", b=1024)
    gT = cx.bf(NPAIR * 1024).rearrange("p (a b) -> p a b", b=1024)
    m1 = cx.mark()
    cx.release(m1 - NPAIR * 1024 * 2)
    wring = Ring("adaw", [cx.bf(8 * 512) for _ in range(2)])
    bring = Ring("adab", [cx.f32(512) for _ in range(2)])
    cx.release(m1)

    ada_vectors(cx, io, consts, layer, [(3, bc["sh"], "sh", False), (4, bc["sc1"], "sc1", True),
                                         (5, bc["g"], "g", False)], wring, bring, [6, 7])
    bc_load(cx, bc["gam"], "gam", io["ln_ffn_g%d" % layer])
    bc_load(cx, bc["bet"], "bet", io["ln_ffn_b%d" % layer])
    cx.dma("sp", cwb, io["cwb%d" % layer], (), ["cwb"])
    w_down = io["w_down%d" % layer]

    ident = consts["ident"]
    cx.memset("pool", hT[:, 0, :, 0:1], 0.0, [("hT", 0, "pad")])
    tb = [6, 7]
    ptiles = []
    for i in range(NT + 1):
        stt_ = {}

        def s0(i=i, stt_=stt_):
            rows = 128 if i < NT else 1
            src = rows_of(x_own, i * 128, 128) if i < NT else halo
            stt_["x"] = B.load_x(src, rows)
            stt_["st"] = B.pro_a(stt_["x"][0], stt_["x"][1], rows)

        def s1(i=i, stt_=stt_):
            rows = 128 if i < NT else 1
            stt_["n"] = B.pro_b1(stt_["x"][0], stt_["x"][1], rows, stt_["st"])

        def s1b(i=i, stt_=stt_):
            rows = 128 if i < NT else 1
            stt_["h"] = hb.next()
            B.pro_b2(stt_["n"], rows, stt_["h"][0][:rows, :], [stt_["h"][1]])

        def s2(i=i, stt_=stt_):
            rows = 128 if i < NT else 1
            h_, hk = stt_["h"]
            b = tb[i % 2]
            pT = cx.bank(b).bitcast(BF16)[:, 0:1024].rearrange("p (a b) -> p a b", b=128)
            cx.transposes([(pT[:, dc, 0:rows], h_[:rows, dc * 128:(dc + 1) * 128]) for dc in range(8)],
                          ident[:rows, :rows], [hk, "ident"], [("ps", b)])
            eng = "act"
            if i < NT:
                half = i // 8
                j0 = 128 * (i % 8) + 1
                cx.copy(eng, hT[:, half, :, j0:j0 + 128], pT, [("ps", b)], [("hT", half, i)])
                if i == 7:
                    cx.copy(eng, hT[:, 1, :, 0:1], pT[:, :, 127:128], [("ps", b)], [("hT", 1, "l")])
                if i == 8:
                    cx.copy(eng, hT[:, 0, :, 1025:1026], pT[:, :, 0:1], [("ps", b)], [("hT", 0, "r")])
            else:
                cx.copy(eng, hT[:, 1, :, 1025:1026], pT[:, :, 0:1], [("ps", b)], [("hT", 1, "r")])

        ptiles.append([s0, s1, s1b, (lambda: None), s2])
    emit_skewed(ptiles)
    hT_keys = {0: [("hT", 0, "pad"), ("hT", 0, "r")] + [("hT", 0, i) for i in range(8)],
               1: [("hT", 1, "l"), ("hT", 1, "r")] + [("hT", 1, i) for i in range(8, 16)]}

    w_up = io["w_up%d" % layer]
    splits = [(255, 512, 0, 257), (512, 1024, 257, 769), (1024, 1281, 769, 1026)]
    order = [(hf, i) for hf in range(2) for i in range(NPAIR)]
    issued = {}
    PF = 2

    def issue(k):
        if k < len(order) and k not in issued:
            wt_, wk_ = wup.next()
            cx.dma("pool", wt_, w_up[order[k][1]], (), [wk_])
            issued[k] = (wt_, wk_)

    for half in range(2):
        for i in range(NPAIR):
            kk = half * NPAIR + i
            for k2 in range(kk, kk + PF + 1):
                issue(k2)
            if half == 0 and i % 2 == 0:
                cx.dma("pool", wd[:, i:i + 2, :], w_down[:, i:i + 2, :], (), [("wd", i), ("wd", i + 1)])
            wt, wk = issued[kk]
            wt4 = wt.rearrange("p (v a b) -> p v a b", v=2, a=8)
            regs = []
            for vg in range(2):
                base = vg * 1536
                pkeys = [("ps", 3 * vg + j) for j in range(3)]
                lst = []
                for dc in range(8):
                    for (p0, p1, h0, h1) in splits:
                        lst.append((cx.ps[:, base + p0:base + p1], wt4[:, vg, dc, :], hT[:, half, dc, h0:h1],
                                    dc == 0, dc == 7))
                cx.mms(lst, [wk] + hT_keys[half], pkeys)
                regs.append((base, pkeys))
            outs = []
            for vg, ring in ((0, av), (1, ag)):
                base, pkeys = regs[vg]
                a, ak = ring.next()
                ci = vg * NPAIR + i
                u = cx.ps
                cx.act(a, u[:, base + 256:base + 1280], AF.Identity, pkeys + ["cwb"], [ak],
                       bias=cwb[:, ci, 3:4], scale=cwb[:, ci, 1:2])
                cx.stt("dve", a, u[:, base + 255:base + 1279], cwb[:, ci, 0:1], a, ALU.mult, ALU.add,
                       pkeys + ["cwb", ak], [ak])
                cx.stt("dve", a, u[:, base + 257:base + 1281], cwb[:, ci, 2:3], a, ALU.mult, ALU.add,
                       pkeys + ["cwb", ak], [ak])
                outs.append((a, ak))
            g_, gk = gl.next()
            cx.act(g_, outs[1][0], AF.Gelu, [outs[1][1]], [gk])
            alias = [("adaw", 0), ("adaw", 1), ("adab", 0), ("adab", 1)] if (half == 0 and i < 10) else []
            cx.tt("pool", gT[:, i, :], outs[0][0], g_, ALU.mult, [outs[0][1], gk], [("gT", i)] + alias)
        for grp in range(2):
            lst = []
            for nc_ in range(NPAIR):
                for tl in range(4):
                    c0 = (grp * 4 + tl) * 128
                    for nh in range(2):
                        lst.append((cx.bank(tl * 2 + nh), gT[:, nc_, c0:c0 + 128], wd[:, nc_, nh * 512:(nh + 1) * 512],
                                    nc_ == 0, nc_ == NPAIR - 1))
            cx.mms(lst, [("gT", i) for i in range(NPAIR)] + [("wd", i) for i in range(NPAIR)],
                   [("ps", b) for b in range(8)])
            items = []
            for tl in range(4):
                r0 = (half * 8 + grp * 4 + tl) * 128
                items.append(dict(y=cx.ps[:, tl * 1024:(tl + 1) * 1024], ykeys=[("ps", 2 * tl), ("ps", 2 * tl + 1)],
                                  src=rows_of(x_own, r0, 128), out=x_out(r0),
                                  halo_out=halo_out if r0 == T - 128 else None,
                                  hook=(lambda r0=r0: tile_hook(r0)) if tile_hook is not None else None))
            B.epilogue_group(items)
    cx.P.barrier()
    cx.release(m0)


def stage_fnet(cx, io, consts, x_own, x_par, x_out, halo_out=None):
    m0 = cx.mark()
    bc = {k: cx.f32(1024) for k in ("sh", "sc1", "g", "gam", "bet")}
    B = Blocks(cx, bc, n_xt=4, n_nt=2, n_tt=4)
    h_all = cx.bf(32 * 1024).rearrange("p (a b) -> p a b", b=1024)
    dring = Ring("dft", [cx.bf(4 * 512) for _ in range(4)])
    FT = [cx.bf(8 * 512).rearrange("p (a b) -> p a b", b=512) for _ in range(2)]
    GTb = cx.bf(8 * 512).rearrange("p (a b) -> p a b", b=512)
    wo = cx.bf(8 * 1024).rearrange("p (a b) -> p a b", b=1024)
    c128 = cx.bf(2 * 128).rearrange("p (a b) -> p a b", b=128)
    wring = Ring("adaw", [cx.bf(8 * 512) for _ in range(2)])
    bring = Ring("adab", [cx.f32(512) for _ in range(2)])

    ada_vectors(cx, io, consts, 0, [(0, bc["sh"], "sh", False), (1, bc["sc1"], "sc1", True),
                                     (2, bc["g"], "g", False)], wring, bring, [6, 7])
    bc_load(cx, bc["gam"], "gam", io["ln_mix_g0"])
    bc_load(cx, bc["bet"], "bet", io["ln_mix_b0"])
    cx.dma("sp", c128, io["c128"], (), ["c128"])
    for q in range(0, 8, 2):
        cx.dma("pool", wo[:, q:q + 2, :], io["fnet_w_out"][:, q:q + 2, :], (), [("wo", q), ("wo", q + 1)])

    ptiles = []
    for et in range(32):
        stt_ = {}

        def s0(et=et, stt_=stt_):
            src = rows_of(x_own, et * 128, 128) if et < NT else rows_of(x_par, (et - NT) * 128, 128)
            stt_["x"] = B.load_x(src)
            stt_["st"] = B.pro_a(stt_["x"][0], stt_["x"][1], 128)

        def s1(et=et, stt_=stt_):
            stt_["n"] = B.pro_b1(stt_["x"][0], stt_["x"][1], 128, stt_["st"])

        def s1b(et=et, stt_=stt_):
            B.pro_b2(stt_["n"], 128, h_all[:, et, :], [("h_all", et)])

        ptiles.append([s0, s1, s1b])
    emit_skewed(ptiles)
    hkeys = [("h_all", et) for et in range(32)]

    dftm = [io["dftc"], io["dfts"]]
    allb = [("ps", b) for b in range(8)]
    ne = 0
    for kc in range(4):
        for cs in range(2):
            for sg in range(8):
                dt_, dk = dring.next()
                cx.dma("sp", dt_, dftm[cs][kc, sg], (), [dk])
                dt3 = dt_.rearrange("p (a b) -> p a b", b=512)
                lst = []
                for sc in range(4):
                    sch = sg * 4 + sc
                    for dc in range(8):
                        lst.append((cx.bank(dc), h_all[:, sch, dc * 128:(dc + 1) * 128], dt3[:, sc, :],
                                    sch == 0, sch == 31))
                cx.mms(lst, [dk] + (hkeys if sg == 0 else []), allb)
            for dc in range(8):
                eng = "act" if ne % 2 == 0 else "dve"
                ne += 1
                cx.copy(eng, FT[cs][:, dc, :], cx.bank(dc), [("ps", dc)], [("FT", cs, dc)])
        for g in range(8):
            cx.mms([(cx.bank(g), c128[:, 0, :], FT[0][:, g, :], True, False),
                    (cx.bank(g), c128[:, 1, :], FT[1][:, g, :], False, True)],
                   ["c128", ("FT", 0, g), ("FT", 1, g)], [("ps", g)])
            eng = "act" if g % 2 == 0 else "dve"
            cx.copy(eng, GTb[:, g, :], cx.bank(g), [("ps", g)], [("GT", g)])
        for tl in range(4):
            lst = []
            for g in range(8):
                for nh in range(2):
                    lst.append((cx.bank(tl * 2 + nh), GTb[:, g, tl * 128:(tl + 1) * 128],
                                wo[:, g, nh * 512:(nh + 1) * 512], g == 0, g == 7))
            cx.mms(lst, [("GT", g) for g in range(8)] + [("wo", g) for g in range(8)],
                   [("ps", tl * 2), ("ps", tl * 2 + 1)])
        items = []
        for tl in range(4):
            r0 = kc * 512 + tl * 128
            items.append(dict(y=cx.ps[:, tl * 1024:(tl + 1) * 1024], ykeys=[("ps", 2 * tl), ("ps", 2 * tl + 1)],
                              src=rows_of(x_own, r0, 128), out=x_out(r0),
                              halo_out=halo_out if r0 == T - 128 else None))
        B.epilogue_group(items)
    cx.P.barrier()
    cx.release(m0)


def stage_attn(cx, io, consts, x_own, x_par, x_out, halo_out=None):
    m0 = cx.mark()
    ident = consts["ident"]
    bc = {}
    B = Blocks(cx, bc, n_xt=3, n_nt=2, n_tt=2)
    hb = Ring("hb", [cx.bf(1024) for _ in range(2)])
    hT = cx.bf(8 * S).rearrange("p (a b) -> p a b", b=S)
    oT = cx.bf(8 * T).rearrange("p (a b) -> p a b", b=T)
    mA = cx.mark()
    bc["sh"] = cx.f32(1024)
    bc["sc1"] = cx.f32(1024)
    wring = Ring("adaw", [cx.bf(8 * 512) for _ in range(2)])
    bring = Ring("adab", [cx.f32(512) for _ in range(2)])
    ada_vectors(cx, io, consts, 1, [(0, bc["sh"], "sh", False), (1, bc["sc1"], "sc1", True)], wring, bring, [6, 7])

    tb = [6, 7]
    ptiles = []
    for et in range(32):
        stt_ = {}

        def s0(et=et, stt_=stt_):
            src = rows_of(x_own, et * 128, 128) if et < NT else rows_of(x_par, (et - NT) * 128, 128)
            stt_["x"] = B.load_x(src)
            stt_["st"] = B.pro_a(stt_["x"][0], stt_["x"][1], 128)

        def s1(et=et, stt_=stt_):
            stt_["n"] = B.pro_b1(stt_["x"][0], stt_["x"][1], 128, stt_["st"])

        def s1b(et=et, stt_=stt_):
            stt_["h"] = hb.next()
            B.pro_b2(stt_["n"], 128, stt_["h"][0], [stt_["h"][1]])

        def s2(et=et, stt_=stt_):
            h_, hk = stt_["h"]
            b = tb[et % 2]
            pTt = cx.bank(b).bitcast(BF16)[:, 0:1024].rearrange("p (a b) -> p a b", b=128)
            cx.transposes([(pTt[:, dc, :], h_[:, dc * 128:(dc + 1) * 128]) for dc in range(8)], ident,
                          [hk, "ident"], [("ps", b)])
            eng = "act"
            cx.copy(eng, hT[:, :, et * 128:(et + 1) * 128], pTt, [("ps", b)], [("hT", et // 4)])

        ptiles.append([s0, s1, s1b, (lambda: None), s2])
    emit_skewed(ptiles)
    hkeys = [("hT", j) for j in range(8)]
    cx.P.barrier()
    cx.release(mA)
    kT = [cx.bf(S) for _ in range(2)]
    qTb = [cx.bf(T) for _ in range(2)]
    qTa = [cx.bf(T) for _ in range(2)]
    V = cx.bf(32 * 129).rearrange("p (a b) -> p a b", b=129)
    wqkv = Ring("wqkv", [cx.bf(3 * 8 * 128) for _ in range(2)])
    pT = Ring("pT", [cx.bf(512) for _ in range(4)])
    dbias = cx.bf(8 * 128).rearrange("p (a b) -> p a b", b=128)
    lamt = cx.f32(4 * 64).rearrange("p (a b) -> p a b", b=64)
    lsum = cx.f32(2)
    lexp = cx.f32(2)
    nlam = cx.f32(1)
    gs = cx.f32(128)
    fin = [{"rl": cx.f32(2), "nl1": cx.f32(1), "o0": cx.f32(128), "o": cx.f32(128), "junk": cx.f32(128),
            "ss": cx.f32(1), "rms": cx.f32(1), "on": cx.bf(128), "key": ("fin", i)} for i in range(4)]
    accS = cx.f32(3 * 387).rearrange("p (a b) -> p a b", b=387)
    cx.dma("sp", dbias, io["dbias"], (), ["dbias"])
    cx.dma("sp", lamt, io["lams"].partition_broadcast(128), (), ["lamt"])
    cx.tt("dve", lamt[:, 0, :], lamt[:, 0, :], lamt[:, 1, :], ALU.mult, ["lamt"], ["lamt"])
    cx.tt("dve", lamt[:, 2, :], lamt[:, 2, :], lamt[:, 3, :], ALU.mult, ["lamt"], ["lamt"])
    cx.P.add("dve", lambda e: e.reduce_sum(out=lsum[:, 0:1], in_=lamt[:, 0, :], axis=AX.X), ["lamt"], ["lsum0"])
    cx.P.add("dve", lambda e: e.reduce_sum(out=lsum[:, 1:2], in_=lamt[:, 2, :], axis=AX.X), ["lamt"], ["lsum1"])
    cx.act(lexp, lsum, AF.Exp, ["lsum0", "lsum1"], ["lexp"])
    cx.stt("dve", nlam, lexp[:, 1:2], -LAM_INIT, lexp[:, 0:1], ALU.add, ALU.subtract, ["lexp"], ["nlam"])
    cx.dma("sp", gs, io["subln_g"].partition_broadcast(128), (), ["gs"])
    cx.ts("dve", gs, gs, 1.0 - LAM_INIT, None, ALU.mult, None, ["gs"], ["gs"])
    for c in range(2):
        cx.dma("sp", qTb[c][64:68, :], io["qaug"][0], (), [("qTb", c, "aug")])
        cx.dma("sp", qTa[c][64:68, :], io["qaug"][1], (), [("qTa", c, "aug")])
    cx.memset("pool", V[:, :, 128:129], 1.0, [("V", "ones")])

    w_in = io["attn_w_in"]
    kaug = io["kaug"]
    def acc_ap(c, qt):
        j = qt * 2 + c
        b = 3 + j // 3
        o = 512 * b + 129 * (j % 3)
        return cx.ps[:, o:o + 129], ("ps", b)

    nev = 0
    nfin = 0
    deferred = []
    for h in range(8):
        wt, wk = wqkv.next()
        cx.dma("pool", wt, w_in[h], (), [wk])
        w4 = wt.rearrange("p (t a b) -> p t a b", t=3, a=8)
        for c in range(2):
            cx.dma("sp", kT[c][64:68, :], kaug[h], (), [("kT", c, "aug")])
        pj = 0
        for tc in range(8):
            b = 6 + pj % 2
            pj += 1
            cx.mms([(cx.bank(b), w4[:, 1, dc, :], hT[:, dc, tc * 512:(tc + 1) * 512], dc == 0, dc == 7)
                    for dc in range(8)], [wk, ("hT", tc)], [("ps", b)])
            eng = "act" if nev % 2 == 0 else "dve"
            nev += 1
            for c in range(2):
                cx.copy(eng, kT[c][0:64, tc * 512:(tc + 1) * 512], cx.bank(b)[c * 64:(c + 1) * 64, :], [("ps", b)],
                        [("kT", c, tc)])
        for tc in range(4):
            b = 6 + pj % 2
            pj += 1
            cx.mms([(cx.bank(b), w4[:, 0, dc, :], hT[:, dc, tc * 512:(tc + 1) * 512], dc == 0, dc == 7)
                    for dc in range(8)], [wk, ("hT", tc)], [("ps", b)])
            for c in range(2):
                cx.act(qTb[c][0:64, tc * 512:(tc + 1) * 512], cx.bank(b)[c * 64:(c + 1) * 64, :], AF.Copy,
                       [("ps", b)], [("qTb", c, tc)], scale=0.125)
                cx.copy("pool", qTa[c][0:64, tc * 512:(tc + 1) * 512], qTb[c][0:64, tc * 512:(tc + 1) * 512],
                        [("qTb", c, tc)], [("qTa", c, tc)])
        for t4 in range(8):
            b = 6 + pj % 2
            pj += 1
            lst = []
            for tl in range(4):
                et = t4 * 4 + tl
                for dc in range(8):
                    lst.append((cx.bank(b)[:, tl * 128:(tl + 1) * 128], hT[:, dc, et * 128:(et + 1) * 128],
                                w4[:, 2, dc, :], dc == 0, dc == 7))
            cx.mms(lst, [wk, ("hT", t4)], [("ps", b)])
            eng = "act" if nev % 2 == 0 else "dve"
            nev += 1
            cx.copy(eng, V[:, t4 * 4:(t4 + 1) * 4, 0:128], cx.bank(b).rearrange("p (a b) -> p a b", b=128),
                    [("ps", b)], [("V", t4)])
        kkeys = {c: [("kT", c, "aug")] + [("kT", c, tc) for tc in range(8)] for c in range(2)}
        for qc in range(4):
            q0 = qc * 512
            pend = []
            tiles = [(kt, c) for kt in range(32) for c in range(2)]

            def emit_score(idx, kt, c):
                sb = idx % 3
                sbank = cx.bank(sb)
                ks = kT[c][:, kt * 128:(kt + 1) * 128]
                rd = kkeys[c] + [("qTb", c, qc), ("qTa", c, qc), ("qTb", c, "aug"), ("qTa", c, "aug")]
                if kt >= 16 or kt > 4 * qc + 3:
                    lst = [(sbank, ks[0:68, :], qTa[c][0:68, q0:q0 + 512], True, True)]
                elif kt < 4 * qc:
                    lst = [(sbank, ks[0:68, :], qTb[c][0:68, q0:q0 + 512], True, True)]
                else:
                    j = kt - 4 * qc
                    lst = []
                    if j > 0:
                        lst.append((sbank[:, 0:128 * j], ks[0:68, :], qTa[c][0:68, q0:q0 + 128 * j], True, True))
                    lst.append((sbank[:, 128 * j:128 * (j + 1)], ks[0:64, :],
                                qTb[c][0:64, q0 + 128 * j:q0 + 128 * (j + 1)], True, False))
                    lst.append((sbank[:, 128 * j:128 * (j + 1)], ident, dbias[:, h, :], False, True))
                    if j < 3:
                        lst.append((sbank[:, 128 * (j + 1):512], ks[0:68, :],
                                    qTb[c][0:68, q0 + 128 * (j + 1):q0 + 512], True, True))
                    rd = rd + ["dbias", "ident"]
                cx.mms(lst, rd, [("ps", sb)])
                return sb

            def emit_rest(idx, kt, c, sb):
                p_, pk = pT.next()
                cx.act(p_, cx.bank(sb), AF.Exp, [("ps", sb)], [pk])
                lst = []
                wk_ = set()
                for ql in range(4):
                    a, akey = acc_ap(c, ql)
                    wk_.add(akey)
                    lst.append((a, p_[:, ql * 128:(ql + 1) * 128], V[:, kt, :],
                                kt == 0 and c == 0 and ql in (0, 2, 3), kt == 31, True))
                cx.mms(lst, [pk, ("V", kt // 4), ("V", "ones")], sorted(wk_))

            sbs = {}
            LOOK = 2
            for idx, (kt, c) in enumerate(tiles):
                sbs[idx] = emit_score(idx, kt, c)
                if idx >= LOOK:
                    j = idx - LOOK
                    emit_rest(j, tiles[j][0], tiles[j][1], sbs[j])
                    if j % 2 == 1 and deferred:
                        deferred.pop(0)()
            for j in range(len(tiles) - LOOK, len(tiles)):
                emit_rest(j, tiles[j][0], tiles[j][1], sbs[j])
            for bb in range(3):
                ncol = 387 if bb < 2 else 258
                cx.copy("dve", accS[:, bb, 0:ncol], cx.ps[:, 512 * (3 + bb):512 * (3 + bb) + ncol], [("ps", 3 + bb)],
                        [("accS", bb)])

            def accs_ap(c, ql):
                j = ql * 2 + c
                return accS[:, j // 3, 129 * (j % 3):129 * (j % 3) + 129], ("accS", j // 3)

            while deferred:
                deferred.pop(0)()
            chunk_thunks = []
            for ql in range(4):
                f = fin[ql]
                fk = f["key"]
                a0, k0 = accs_ap(0, ql)
                a1, k1 = accs_ap(1, ql)
                rk = sorted({k0, k1})
                qt = qc * 4 + ql

                def fa(f=f, fk=fk, a0=a0, a1=a1, rk=rk):
                    cx.P.add("dve", lambda e: e.reciprocal(out=f["rl"][:, 0:1], in_=a0[:, 128:129]), rk, [(fk, "rl0")])
                    cx.P.add("dve", lambda e: e.reciprocal(out=f["rl"][:, 1:2], in_=a1[:, 128:129]), rk, [(fk, "rl1")])
                    cx.tt("dve", f["nl1"], f["rl"][:, 1:2], nlam, ALU.mult, [(fk, "rl1"), "nlam"], [(fk, "nl1")])
                    cx.ts("dve", f["o0"], a0[:, 0:128], f["rl"][:, 0:1], None, ALU.mult, None, rk + [(fk, "rl0")],
                          [(fk, "o0")])
                    cx.stt("dve", f["o"], a1[:, 0:128], f["nl1"], f["o0"], ALU.mult, ALU.add,
                           rk + [(fk, "nl1"), (fk, "o0")], [(fk, "o")])

                def fb(f=f, fk=fk):
                    cx.act(f["junk"], f["o"], AF.Square, [(fk, "o")], [(fk, "junk"), (fk, "ss")], accum_out=f["ss"])
                    cx.act(f["rms"], f["ss"], AF.Ln, [(fk, "ss"), "epsc"], [(fk, "rms")], bias=cx.eps, scale=1.0 / 128.0)
                    cx.act(f["rms"], f["rms"], AF.Exp, [(fk, "rms")], [(fk, "rms")], scale=-0.5)

                def fc(f=f, fk=fk, h=h, qt=qt):
                    cx.stt("dve", f["on"], f["o"], f["rms"], gs, ALU.mult, ALU.mult, [(fk, "o"), (fk, "rms"), "gs"],
                           [(fk, "on")])
                    b = 6 + qt % 2
                    pTt = cx.bank(b).bitcast(BF16)[:, 0:128]
                    cx.transposes([(pTt, f["on"])], ident, [(fk, "on"), "ident"], [("ps", b)])
                    cx.copy("dve", oT[:, h, qt * 128:(qt + 1) * 128], pTt, [("ps", b)], [("oT", h, qt)])

                chunk_thunks.append((fa, fb, fc))
            A_, B_, C_ = zip(*chunk_thunks)
            deferred.extend([A_[0], A_[1], B_[0], A_[2], B_[1], C_[0], A_[3], B_[2], C_[1], B_[3], C_[2], C_[3]])
    while deferred:
        deferred.pop(0)()

    cx.P.barrier()
    cx.release(mA)
    bc["g"] = cx.f32(1024)
    bc["gam"] = cx.f32(1024)
    bc["bet"] = cx.f32(1024)
    wo = cx.bf(8 * 1024).rearrange("p (a b) -> p a b", b=1024)
    wring = Ring("adaw2", [cx.bf(8 * 512) for _ in range(2)])
    bring = Ring("adab2", [cx.f32(512) for _ in range(2)])
    ada_vectors(cx, io, consts, 1, [(2, bc["g"], "g", False)], wring, bring, [6, 7])
    bc_load(cx, bc["gam"], "gam", io["ln_mix_g1"])
    bc_load(cx, bc["bet"], "bet", io["ln_mix_b1"])
    for q in range(0, 8, 2):
        cx.dma("pool", wo[:, q:q + 2, :], io["attn_w_out"][:, q:q + 2, :], (), [("wo", q), ("wo", q + 1)])
    otiles = []
    for qt in range(NT):
        stt_ = {}

        def o0(qt=qt, stt_=stt_):
            b0 = (qt % 2) * 2
            lst = []
            for h in range(8):
                for nh in range(2):
                    lst.append((cx.bank(b0 + nh), oT[:, h, qt * 128:(qt + 1) * 128], wo[:, h, nh * 512:(nh + 1) * 512],
                                h == 0, h == 7))
            stt_["x"] = B.load_x(rows_of(x_own, qt * 128, 128))
            cx.mms(lst, [("oT", h, qt) for h in range(8)] + [("wo", g) for g in range(8)], [("ps", b0), ("ps", b0 + 1)])
            stt_["t"] = B.epi_a(cx.ps[:, b0 * 512:(b0 + 2) * 512], [("ps", b0), ("ps", b0 + 1)])

        def o1(qt=qt, stt_=stt_):
            stt_["st"] = B.epi_b1(stt_["t"][0], stt_["t"][1], stt_["x"][0], stt_["x"][1])

        def o2(qt=qt, stt_=stt_):
            B.epi_b2(stt_["t"][0], stt_["t"][1], stt_["st"], x_out(qt * 128), halo_out if qt == NT - 1 else None)

        otiles.append([o0, o1, o2])
    emit_skewed(otiles)
    cx.P.barrier()
    cx.release(m0)


ARENA_BYTES = 206 * 1024

IN_SPECS = {
    "ident": ([128, 128], BF16),
    "c_t": ([128, 8], F32),
    "ada_w0": ([12, 128, 4096], F32), "ada_b0": ([6 * D], F32),
    "ada_w1": ([12, 128, 4096], F32), "ada_b1": ([6 * D], F32),
    "ln_mix_g0": ([D], F32), "ln_mix_b0": ([D], F32), "ln_ffn_g0": ([D], F32), "ln_ffn_b0": ([D], F32),
    "ln_mix_g1": ([D], F32), "ln_mix_b1": ([D], F32), "ln_ffn_g1": ([D], F32), "ln_ffn_b1": ([D], F32),
    "fnet_w_out": ([128, 8, D], F32), "c128": ([128, 2, 128], BF16), "dftc": ([4, 8, 128, 2048], BF16), "dfts": ([4, 8, 128, 2048], BF16),
    "w_up0": ([NPAIR, 128, 2048], F32), "cwb0": ([128, 44, 4], F32), "w_down0": ([128, NPAIR, D], F32),
    "w_up1": ([NPAIR, 128, 2048], F32), "cwb1": ([128, 44, 4], F32), "w_down1": ([128, NPAIR, D], F32),
    "attn_w_in": ([8, 128, 3072], F32), "attn_w_out": ([128, 8, D], F32), "lams": ([4, 64], F32),
    "subln_g": ([128], F32), "kaug": ([8, 4, S], BF16), "qaug": ([2, 4, T], BF16), "dbias": ([128, 8, 128], BF16),
}

STAGE_INPUTS = {
    "fnet": ["ident", "c_t", "ada_w0", "ada_b0", "ln_mix_g0", "ln_mix_b0", "fnet_w_out", "c128", "dftc", "dfts"],
    "ffn0": ["ident", "c_t", "ada_w0", "ada_b0", "ln_ffn_g0", "ln_ffn_b0", "w_up0", "cwb0", "w_down0"],
    "attn": ["ident", "c_t", "ada_w1", "ada_b1", "ln_mix_g1", "ln_mix_b1", "attn_w_in", "attn_w_out", "lams",
             "subln_g", "kaug", "qaug", "dbias"],
    "ffn1": ["ident", "c_t", "ada_w1", "ada_b1", "ln_ffn_g1", "ln_ffn_b1", "w_up1", "cwb1", "w_down1"],
}


def build_stage_program(stage):
    nc = bass.Bass("TRN2", target_bir_lowering=False)
    io = {}
    for name in STAGE_INPUTS[stage]:
        shape, dt = IN_SPECS[name]
        io[name] = nc.dram_tensor(name, shape, dt, kind="ExternalInput").ap()
    x_own = nc.dram_tensor("x_own", [T, D], F32, kind="ExternalInput").ap()
    if stage in ("fnet", "attn"):
        x_par = nc.dram_tensor("x_par", [T, D], F32, kind="ExternalInput").ap()
    else:
        halo = nc.dram_tensor("halo", [1, D], F32, kind="ExternalInput").ap()
    x_out = nc.dram_tensor("x_out", [T, D], F32, kind="ExternalOutput").ap()
    with contextlib.ExitStack() as es:
        arena = es.enter_context(nc.sbuf_tensor("arena", [128, ARENA_BYTES], U8))
        ps = es.enter_context(nc.psum_tensor("ps", [128, 4096], F32))
        cx = Cx(nc, arena, ARENA_BYTES, ps)
        consts = load_consts(cx, io)
        ada_setup(cx, io, consts)
        xo = lambda r0: (x_out[r0:r0 + 128, :], ("xout", r0))
        if stage == "fnet":
            stage_fnet(cx, io, consts, ("d", x_own), ("d", x_par), xo)
        elif stage == "attn":
            stage_attn(cx, io, consts, ("d", x_own), ("d", x_par), xo)
        else:
            stage_ffn(cx, io, consts, int(stage[-1]), ("d", x_own), ("d", halo), xo)
        cx.P.build()
    return nc


_CONST_CACHE = {}


def _dft_consts(parity):
    key = ("dft", parity)
    if key in _CONST_CACHE:
        return _CONST_CACHE[key]
    e = np.arange(S)
    pos = np.where(e < T, e, 6143 - e)
    k = np.arange(T)
    if parity == 0:
        gp, gk = pos, k
    else:
        gp, gk = 4095 - pos, 4095 - k
    prod = (gp[:, None].astype(np.int64) * gk[None, :].astype(np.int64)) % S
    ang = (2.0 * np.pi / S) * prod.astype(np.float64)
    out = []
    for fn in (np.cos, np.sin):
        m = (fn(ang) / 64.0).astype(np.float32)
        m = m.reshape(8, 4, 128, 4, 512)
        out.append(np.ascontiguousarray(m.transpose(3, 0, 2, 1, 4).astype(NPBF)).reshape(4, 8, 128, 2048))
    _CONST_CACHE[key] = out
    return out


def _static_consts():
    if "static" in _CONST_CACHE:
        return _CONST_CACHE["static"]
    c = {}
    c["ident"] = np.eye(128, dtype=np.float32).astype(NPBF)
    dd = np.arange(128)
    ang = 2.0 * np.pi * ((dd[:, None] * dd[None, :]) % 128) / 128.0
    sc = 1.0 / math.sqrt(128.0)
    c128 = np.stack([np.cos(ang) * sc, -np.sin(ang) * sc], axis=1)
    c["c128"] = c128.astype(np.float32).astype(NPBF)
    slopes = np.array([2.0 ** (-(h + 1)) for h in range(8)], dtype=np.float64)
    e = np.arange(S)
    pos = np.where(e < T, e, 6143 - e)
    k0 = (pos // 128) * 128
    kl = pos - k0
    kaug = np.empty((8, 4, S), dtype=np.float64)
    for h in range(8):
        kaug[h, 0] = slopes[h] * k0
        kaug[h, 1] = slopes[h] * kl
        kaug[h, 2] = -slopes[h]
        kaug[h, 3] = -slopes[h]
    c["kaug"] = kaug.astype(np.float32).astype(NPBF)
    j = np.arange(T)
    q0 = (j // 256) * 256
    ql = j - q0
    qb = np.stack([np.ones(T), np.ones(T), q0, ql]).astype(np.float64)
    c["qaug"] = np.stack([qb, -qb]).astype(np.float32).astype(NPBF)
    dist = np.abs(dd[:, None] - dd[None, :]).astype(np.float64)
    db = np.stack([-slopes[h] * dist for h in range(8)], axis=1)
    c["dbias"] = db.astype(np.float32).astype(NPBF)
    _CONST_CACHE["static"] = c
    return c


def _prep_weights(inp):
    L = [
        dict(ada_w=inp["l0_ada_w"], ada_b=inp["l0_ada_b"], ln_mix_g=inp["l0_ln_mix_g"], ln_mix_b=inp["l0_ln_mix_b"],
             ffn_w_up=inp["l0_ffn_w_up"], ffn_conv_w=inp["l0_ffn_conv_w"], ffn_conv_b=inp["l0_ffn_conv_b"],
             ffn_w_down=inp["l0_ffn_w_down"], ln_ffn_g=inp["l0_ln_ffn_g"], ln_ffn_b=inp["l0_ln_ffn_b"]),
        dict(ada_w=inp["l1_ada_w"], ada_b=inp["l1_ada_b"], ln_mix_g=inp["l1_ln_mix_g"], ln_mix_b=inp["l1_ln_mix_b"],
             ffn_w_up=inp["l1_ffn_w_up"], ffn_conv_w=inp["l1_ffn_conv_w"], ffn_conv_b=inp["l1_ffn_conv_b"],
             ffn_w_down=inp["l1_ffn_w_down"], ln_ffn_g=inp["l1_ln_ffn_g"], ln_ffn_b=inp["l1_ln_ffn_b"]),
    ]
    w = {}
    for i in range(2):
        P = L[i]
        w["ada_w%d" % i] = np.ascontiguousarray(P["ada_w"].reshape(8, 128, 12, 512).transpose(2, 1, 0, 3)).reshape(12, 128, 4096)
        w["ada_b%d" % i] = np.ascontiguousarray(P["ada_b"])
        for n in ("ln_mix_g", "ln_mix_b", "ln_ffn_g", "ln_ffn_b"):
            w["%s%d" % (n, i)] = np.ascontiguousarray(P[n])
        wu = P["ffn_w_up"].reshape(8, 128, 2, NPAIR, 128)
        w["w_up%d" % i] = np.ascontiguousarray(wu.transpose(3, 1, 2, 0, 4)).reshape(NPAIR, 128, 2048)
        wdn = P["ffn_w_down"].reshape(NPAIR, 128, D)
        w["w_down%d" % i] = np.ascontiguousarray(wdn.transpose(1, 0, 2))
        cw = P["ffn_conv_w"]
        cb = P["ffn_conv_b"]
        for par in range(2):
            taps = cw if par == 0 else cw[::-1]
            t = np.concatenate([taps, cb[None, :]], axis=0)
            t = t.reshape(4, 44, 128).transpose(2, 1, 0)
            w["cwb%d_%d" % (i, par)] = np.ascontiguousarray(t)
    w["fnet_w_out"] = np.ascontiguousarray(inp["l0_fnet_w_out"].reshape(8, 128, D).transpose(1, 0, 2))
    w["attn_w_out"] = np.ascontiguousarray(inp["l1_attn_w_out"].reshape(8, 128, D).transpose(1, 0, 2))
    wi = inp["l1_attn_w_in"].reshape(8, 128, 3, 8, 128)
    w["attn_w_in"] = np.ascontiguousarray(wi.transpose(3, 1, 2, 0, 4)).reshape(8, 128, 3072)
    w["lams"] = np.ascontiguousarray(np.stack([inp["l1_attn_lambda_q1"], inp["l1_attn_lambda_k1"],
                                               inp["l1_attn_lambda_q2"], inp["l1_attn_lambda_k2"]]))
    w["subln_g"] = np.ascontiguousarray(inp["l1_attn_subln_g"])
    return w


def _core_inputs(stage, r, w, cst, c):
    b, par = r // 2, r % 2
    m = {}
    for name in (STAGE_INPUTS[stage] if stage is not None else list(IN_SPECS)):
        if name == "c_t":
            m[name] = np.ascontiguousarray(c[b].reshape(8, 128).T)
        elif name.startswith("cwb"):
            m[name] = w["%s_%d" % (name, par)]
        elif name == "dftc":
            m[name] = _dft_consts(par)[0]
        elif name == "dfts":
            m[name] = _dft_consts(par)[1]
        elif name in cst:
            m[name] = cst[name]
        else:
            m[name] = w[name]
    return m


def _to_local(xfull, r):
    b, par = r // 2, r % 2
    xb = xfull[b]
    if par == 0:
        own, partner = xb[:T], xb[::-1][:T]
    else:
        own, partner = xb[::-1][:T], xb[:T]
    return np.ascontiguousarray(own), np.ascontiguousarray(partner)


def _from_local(outs):
    full = np.empty((4, S, D), dtype=np.float32)
    for r, o in enumerate(outs):
        b, par = r // 2, r % 2
        if par == 0:
            full[b, :T] = o
        else:
            full[b, T:] = o[::-1]
    return full


_PROGS = {}


def _run_stage(stage, xfull, w, cst, c):
    if stage not in _PROGS:
        _PROGS[stage] = build_stage_program(stage)
    nc = _PROGS[stage]
    in_maps = []
    for r in range(8):
        m = _core_inputs(stage, r, w, cst, c)
        own, partner = _to_local(xfull, r)
        m["x_own"] = own
        if stage in ("fnet", "attn"):
            m["x_par"] = partner
        else:
            m["halo"] = np.ascontiguousarray(partner[T - 1:T])
        in_maps.append(m)
    res = run_bass_kernel_spmd(nc, in_maps, core_ids=list(range(8)))
    return _from_local([np.asarray(res.results[r]["x_out"]) for r in range(8)])


RG = [[0, 1], [2, 3], [4, 5], [6, 7]]


def build_fused_program():
    nc = bass.Bass("TRN2", target_bir_lowering=False)
    io = {}
    for name, (shape, dt) in IN_SPECS.items():
        io[name] = nc.dram_tensor(name, shape, dt, kind="ExternalInput").ap()
    x_own = nc.dram_tensor("x_own", [T, D], F32, kind="ExternalInput").ap()
    x_par = nc.dram_tensor("x_par", [T, D], F32, kind="ExternalInput").ap()
    mask_in = nc.dram_tensor("mask", [128, 2], F32, kind="ExternalInput").ap()
    x_out = nc.dram_tensor("x_out", [T, D], F32, kind="ExternalOutput").ap()
    t_xs1 = nc.dram_tensor("xs1", [T, D], F32)
    t_hal1 = nc.dram_tensor("hal1", [1, D], F32)
    t_hg1 = nc.dram_tensor("hg1", [2, D], F32)
    t_xs2 = [nc.dram_tensor("xs2_%d" % i, [128, D], F32) for i in range(NT)]
    t_xg2 = [nc.dram_tensor("xg2_%d" % i, [256, D], F32) for i in range(NT)]
    t_xs3 = nc.dram_tensor("xs3", [T, D], F32)
    t_hal3 = nc.dram_tensor("hal3", [1, D], F32)
    t_hg3 = nc.dram_tensor("hg3", [2, D], F32)
    xs1, hal1, hg1, xs3, hal3, hg3 = (t.ap() for t in (t_xs1, t_hal1, t_hg1, t_xs3, t_hal3, t_hg3))
    xs2 = [t.ap() for t in t_xs2]
    xg2 = [t.ap() for t in t_xg2]

    def gather(cx, tin, tout, reads, writes):
        cx.P.add("pool", lambda e: e.collective_compute("AllGather", ALU.bypass, replica_groups=RG,
                                                        ins=[tin.ap().opt()], outs=[tout.ap().opt()]),
                 reads, writes, cc=True)

    with contextlib.ExitStack() as es:
        arena = es.enter_context(nc.sbuf_tensor("arena", [128, ARENA_BYTES], U8))
        ps = es.enter_context(nc.psum_tensor("ps", [128, 4096], F32))
        cx = Cx(nc, arena, ARENA_BYTES, ps)
        consts = load_consts(cx, io)
        ada_setup(cx, io, consts)
        mask = cx.f32(2)
        cx.dma("sp", mask, mask_in, (), ["mask"])

        stage_fnet(cx, io, consts, ("d", x_own), ("d", x_par),
                   lambda r0: (xs1[r0:r0 + 128, :], ("xs1", r0)), halo_out=hal1)
        gather(cx, t_hal1, t_hg1, ["halo_out"], ["hg1"])
        x_own2 = ("f", lambda r0, n: ("d", xs1[r0:r0 + n, :], [("xs1", r0)]))
        halo2 = ("m", hg1[0:1, :], hg1[1:2, :], mask, ["hg1"])

        def xo2(r0):
            return xs2[r0 // 128], ("xs2", r0)

        pend = []

        def hook2(r0):
            pend.append(r0)
            if len(pend) > 1:
                p = pend.pop(0)
                gather(cx, t_xs2[p // 128], t_xg2[p // 128], [("xs2", p)], [("xg2", p)])

        stage_ffn(cx, io, consts, 0, x_own2, halo2, xo2, tile_hook=hook2)
        for p in pend:
            gather(cx, t_xs2[p // 128], t_xg2[p // 128], [("xs2", p)], [("xg2", p)])

        def own3(r0, n):
            return ("d", xs2[r0 // 128], [("xs2", r0)])

        def par3(r0, n):
            g = xg2[r0 // 128]
            return ("m", g[0:128, :], g[128:256, :], mask, [("xg2", r0)])

        stage_attn(cx, io, consts, ("f", own3), ("f", par3),
                   lambda r0: (xs3[r0:r0 + 128, :], ("xs3", r0)), halo_out=hal3)
        gather(cx, t_hal3, t_hg3, ["halo_out"], ["hg3"])
        x_own4 = ("f", lambda r0, n: ("d", xs3[r0:r0 + n, :], [("xs3", r0)]))
        halo4 = ("m", hg3[0:1, :], hg3[1:2, :], mask, ["hg3"])
        stage_ffn(cx, io, consts, 1, x_own4, halo4, lambda r0: (x_out[r0:r0 + 128, :], ("xout", r0)))
        cx.P.build()
    return nc


_FUSED = []


def kernel(**inputs):
    inp = {k: np.asarray(v) for k, v in inputs.items()}
    w = _prep_weights(inp)
    cst = _static_consts()
    x = np.ascontiguousarray(inp["x"], dtype=np.float32)
    c = inp["c"]
    if not _FUSED:
        _FUSED.append(build_fused_program())
    nc = _FUSED[0]
    in_maps = []
    for r in range(8):
        m = _core_inputs(None, r, w, cst, c)
        own, partner = _to_local(x, r)
        m["x_own"] = own
        m["x_par"] = partner
        mk = np.zeros((128, 2), np.float32)
        mk[:, 1 - (r % 2)] = 1.0
        m["mask"] = mk
        in_maps.append(m)
    res = run_bass_kernel_spmd(nc, in_maps, core_ids=list(range(8)))
    return _from_local([np.asarray(res.results[r]["x_out"]) for r in range(8)])
```
